# Optimizing a Trainium2 kernel written in Bass

```python
import jax, jax.numpy as jnp
from jax import lax
import numpy as np

D_MODEL = 1024
BATCH = 8
SEQ = 2048
DEPTH = 2
DEC_BATCH = 128
DEC_SEQ = 4
PAST_LEN = 16384
PAGE_SIZE = 128

HEAD_DIM = 64
W_A = 3 * D_MODEL // 8
W_B = 3 * D_MODEL // 8
W_C = D_MODEL // 4
A_HEADS = W_A // HEAD_DIM
B_HEADS = W_B // HEAD_DIM
C_HEADS = W_C // HEAD_DIM
MIX_WIDTH = W_A + W_B + W_C
A_KERNEL = 31
B_KERNEL = 3
CHUNK = 128
EPS = 1e-6
SPLIT_SIZES = (W_A, W_A, W_A, W_B, W_B, W_B, W_B, W_C, W_C, W_C)
IN_WIDTH = sum(SPLIT_SIZES)

kernel_name = "hybrid_conformer_shortconv_chunkmlp_step"


def rms_norm(x, g):
    xf = x.astype(jnp.float32)
    y = xf * lax.rsqrt(jnp.mean(xf * xf, axis=-1, keepdims=True) + EPS)
    return (y * g.astype(jnp.float32)).astype(x.dtype)


def layer_norm(x, g, b):
    xf = x.astype(jnp.float32)
    mu = jnp.mean(xf, axis=-1, keepdims=True)
    var = jnp.mean(jnp.square(xf - mu), axis=-1, keepdims=True)
    y = (xf - mu) * lax.rsqrt(var + EPS)
    return (y * g.astype(jnp.float32) + b.astype(jnp.float32)).astype(x.dtype)


def causal_dwconv(buf, x, w):
    full = jnp.concatenate([buf.astype(x.dtype), x], axis=1)
    k, c = w.shape
    y = lax.conv_general_dilated(full, w.astype(x.dtype)[:, None, :], window_strides=(1,),
                                 padding='VALID', dimension_numbers=('NWC', 'WIO', 'NWC'),
                                 feature_group_count=c)
    return y, full[:, full.shape[1] - (k - 1):, :]


def chunk_mix_prompt(v, ws, bs):
    n, t, _ = v.shape
    mask = jnp.tril(jnp.ones((CHUNK, CHUNK), ws.dtype))
    wm = (ws * mask).astype(v.dtype)
    vh = v.reshape(n, t // CHUNK, CHUNK, C_HEADS, HEAD_DIM)
    out = jnp.einsum('hts,ncshd->ncthd', wm, vh) + bs.astype(v.dtype).T[None, None, :, :, None]
    return out.reshape(n, t, W_C)


def chunk_mix_new_chunk(v, ws, bs):
    n, t, _ = v.shape
    mask = jnp.tril(jnp.ones((CHUNK, CHUNK), ws.dtype))
    wm = (ws * mask)[:, :t, :t].astype(v.dtype)
    vh = v.reshape(n, t, C_HEADS, HEAD_DIM)
    out = jnp.einsum('hts,nshd->nthd', wm, vh) + bs[:, :t].astype(v.dtype).T[None, :, :, None]
    return out.reshape(n, t, W_C)


def hybrid_layer(x, a_buf, b_buf, chunk_fn, norm_g, w_in, a_conv_w, a_conv_b, a_ln_g, a_ln_b,
                 b_conv_w, c_ln_g, c_ln_b, c_ws, c_bs, w_out):
    h = rms_norm(x, norm_g)
    p = jnp.einsum('ntd,de->nte', h, w_in.astype(h.dtype))
    idx = np.cumsum(SPLIT_SIZES)[:-1].tolist()
    a_val, a_gate, a_z, b_h, b_b, b_c, b_z, c_u, c_v, c_z = jnp.split(p, idx, axis=-1)
    a = a_val * jax.nn.sigmoid(a_gate)
    a_conv, new_a_buf = causal_dwconv(a_buf, a, a_conv_w)
    a_conv = a_conv + a_conv_b.astype(a_conv.dtype)
    y_a = jax.nn.silu(layer_norm(a_conv, a_ln_g, a_ln_b)) * jax.nn.silu(a_z)
    z = b_c * b_h
    b_conv, new_b_buf = causal_dwconv(b_buf, z, b_conv_w)
    y_b = b_b * b_conv * jax.nn.silu(b_z)
    vn = layer_norm(c_v, c_ln_g, c_ln_b)
    y_c = c_u * chunk_fn(vn, c_ws, c_bs) * jax.nn.silu(c_z)
    mix = jnp.concatenate([y_a, y_b, y_c], axis=-1)
    out = jnp.einsum('nte,ed->ntd', mix, w_out.astype(mix.dtype))
    return x + out, new_a_buf, new_b_buf, vn


def setup_inputs(seed: int = 0) -> dict:
    key = jax.random.key(seed)
    ks = jax.random.split(key, 20)
    f32 = jnp.float32
    nrm = lambda k, s: jax.random.normal(k, s, f32)
    return {
        "x_prompt": nrm(ks[0], (BATCH, SEQ, D_MODEL)),
        "x_sample": nrm(ks[1], (DEC_BATCH, DEC_SEQ, D_MODEL)),
        "state_a_conv": 0.5 * nrm(ks[2], (DEPTH, DEC_BATCH, A_KERNEL - 1, W_A)),
        "state_b_conv": 0.5 * nrm(ks[3], (DEPTH, DEC_BATCH, B_KERNEL - 1, W_B)),
        "norm_g": 1.0 + 0.05 * nrm(ks[4], (DEPTH, D_MODEL)),
        "w_in": nrm(ks[5], (DEPTH, D_MODEL, IN_WIDTH)) * D_MODEL ** -0.5,
        "a_conv_w": nrm(ks[6], (DEPTH, A_KERNEL, W_A)) * A_KERNEL ** -0.5,
        "a_conv_b": 0.02 * nrm(ks[7], (DEPTH, W_A)),
        "a_ln_g": 1.0 + 0.05 * nrm(ks[8], (DEPTH, W_A)),
        "a_ln_b": 0.02 * nrm(ks[9], (DEPTH, W_A)),
        "b_conv_w": nrm(ks[10], (DEPTH, B_KERNEL, W_B)) * B_KERNEL ** -0.5,
        "c_ln_g": 1.0 + 0.05 * nrm(ks[11], (DEPTH, W_C)),
        "c_ln_b": 0.02 * nrm(ks[12], (DEPTH, W_C)),
        "c_ws": nrm(ks[13], (DEPTH, C_HEADS, CHUNK, CHUNK)) * CHUNK ** -0.5,
        "c_bs": 1.0 + 0.1 * nrm(ks[14], (DEPTH, C_HEADS, CHUNK)),
        "w_out": nrm(ks[15], (DEPTH, MIX_WIDTH, D_MODEL)) * MIX_WIDTH ** -0.5,
        "final_g": 1.0 + 0.05 * nrm(ks[16], (D_MODEL,)),
    }


def reference(x_prompt, x_sample, state_a_conv, state_b_conv, norm_g, w_in, a_conv_w, a_conv_b,
              a_ln_g, a_ln_b, b_conv_w, c_ln_g, c_ln_b, c_ws, c_bs, w_out, final_g):
    xp, xs = x_prompt, x_sample
    n_p = xp.shape[0]
    pa, pb, sa, sb, sv = [], [], [], [], []
    for l in range(DEPTH):
        lw = (norm_g[l], w_in[l], a_conv_w[l], a_conv_b[l], a_ln_g[l], a_ln_b[l], b_conv_w[l],
              c_ln_g[l], c_ln_b[l], c_ws[l], c_bs[l], w_out[l])
        a0 = jnp.zeros((n_p, A_KERNEL - 1, W_A), xp.dtype)
        b0 = jnp.zeros((n_p, B_KERNEL - 1, W_B), xp.dtype)
        xp, na, nb, _ = hybrid_layer(xp, a0, b0, chunk_mix_prompt, *lw)
        pa.append(na)
        pb.append(nb)
        xs, na, nb, vn = hybrid_layer(xs, state_a_conv[l], state_b_conv[l], chunk_mix_new_chunk, *lw)
        sa.append(na)
        sb.append(nb)
        sv.append(vn)
    y_prompt = rms_norm(xp, final_g)
    y_sample = rms_norm(xs, final_g)
    return (y_prompt, y_sample, jnp.stack(pa), jnp.stack(pb), jnp.stack(sa), jnp.stack(sb), jnp.stack(sv))
```

```python
import numpy as np
from contextlib import ExitStack

import concourse.bass as bass
import concourse.mybir as mybir
from concourse.bass_utils import run_bass_kernel_spmd

F32 = mybir.dt.float32
BF16 = mybir.dt.bfloat16
AF = mybir.ActivationFunctionType
ALU = mybir.AluOpType

NCORES = 8
D = 1024
INW = 3456
WA = 384
WC = 256
DEPTH = 2
SEQ = 2048
NSEG = 16
DSEQ = 4
NS = NSEG * DSEQ
KA = 31
KB = 3
NB = 256
NPB = SEQ // NB
EPS = 1e-6
C_AV, C_AG, C_AZ = 0, 384, 768
C_BH, C_BB, C_BC, C_BZ = 1152, 1536, 1920, 2304
C_CU, C_CV, C_CZ = 2688, 2944, 3200
PIECES = [(C_AV, 384), (C_AG, 384), (C_BH, 384), (C_BC, 384), (C_CV, 256), (C_BB, 384), (C_BZ, 384),
          (C_CU, 256), (C_CZ, 256), (C_AZ, 384)]


def piece_of(col):
    for i, (c0, w) in enumerate(PIECES):
        if c0 <= col < c0 + w:
            return i
    raise ValueError(col)


class Op:
    __slots__ = ("eng", "fn", "deps", "multi", "dma", "idx", "sig", "prewait")

    def __init__(self, eng, fn, multi, dma, idx):
        self.eng, self.fn, self.multi, self.dma, self.idx = eng, fn, multi, dma, idx
        self.deps = set()
        self.sig = None
        self.prewait = None


class Sched:
    COMPUTE = ("pe", "act", "dve", "pool")
    MAXV = 4000
    NDMA = {"sp": 20, "pool": 8, "act": 4}

    def __init__(self):
        self.ops = []
        self.lastw = {}
        self.rd = {}

    def add(self, eng, fn, reads=(), writes=(), multi=False, dma=False):
        op = Op(eng, fn, multi, dma, len(self.ops))
        psr = [k for k in reads if isinstance(k, tuple) and k[0] == "psb"]
        if psr:
            reads = [k for k in reads if not (isinstance(k, tuple) and k[0] == "psb")]
            writes = list(writes) + psr
        for k in reads:
            w = self.lastw.get(k)
            if w is not None:
                op.deps.add(w)
        for k in writes:
            w = self.lastw.get(k)
            if w is not None:
                op.deps.add(w)
            op.deps.update(self.rd.get(k, ()))
        for k in reads:
            self.rd.setdefault(k, []).append(op.idx)
        for k in writes:
            self.lastw[k] = op.idx
            self.rd[k] = []
        op.deps.discard(op.idx)
        self.ops.append(op)
        return op

    def emit(self, nc, stack, block):
        ops = self.ops
        for op in ops:
            latest = {}
            keep = set()
            for p in op.deps:
                po = ops[p]
                if po.dma:
                    keep.add(p)
                    continue
                if po.eng == "pe" and op.eng == "pe" and not op.dma:
                    continue
                if po.eng not in latest or latest[po.eng] < p:
                    latest[po.eng] = p
            keep.update(latest.values())
            op.deps = keep
        needed = set()
        for op in ops:
            needed.update(op.deps)
        queues = {}
        for op in ops:
            queues.setdefault(op.eng, []).append(op)
        sems = {}

        def getsem(name):
            if name not in sems:
                sems[name] = stack.enter_context(nc.semaphore(name))
            return sems[name]

        for eng in self.COMPUTE:
            cnt = 0
            for op in queues.get(eng, []):
                if op.dma:
                    continue
                if op.idx in needed:
                    e, v = divmod(cnt, self.MAXV)
                    getsem("c_%s_%d" % (eng, e))
                    op.sig = ("c_%s_%d" % (eng, e), v + 1, 1)
                    cnt += 1
        all_dma_final = {}
        for eng, q in queues.items():
            n = self.NDMA.get(eng, 4)
            uses = [0] * n
            rr = 0
            for op in q:
                if not op.dma:
                    continue
                s = "d_%s_%d" % (eng, rr)
                getsem(s)
                if uses[rr] > 0:
                    op.prewait = (s, 16 * uses[rr])
                uses[rr] += 1
                op.sig = (s, 16 * uses[rr], 16)
                all_dma_final[s] = 16 * uses[rr]
                rr = (rr + 1) % n
        handles = {"pe": block.tensor, "act": block.scalar, "dve": block.vector, "pool": block.gpsimd,
                   "sp": block.sync}
        order = ["sp", "pool", "act", "dve", "pe"]
        for eng in order:
            q = queues.get(eng, [])

            def body(e, q=q, eng=eng):
                seen = {}
                for op in q:
                    waits = {}
                    for p in op.deps:
                        s, v, _ = ops[p].sig
                        if waits.get(s, 0) < v:
                            waits[s] = v
                    if op.prewait is not None:
                        s, v = op.prewait
                        if waits.get(s, 0) < v:
                            waits[s] = v
                    wl = [(s, v) for s, v in waits.items() if seen.get(s, 0) < v]
                    for s, v in wl:
                        seen[s] = v
                    attach = None
                    if wl and not op.multi:
                        attach = wl.pop()
                    for s, v in wl:
                        e.wait_ge(sems[s], v)
                    ins = op.fn(e)
                    if attach is not None:
                        ins._wait_ge(sems[attach[0]], attach[1])
                    if op.sig is not None:
                        ins.then_inc(sems[op.sig[0]], op.sig[2])
                if eng == "sp":
                    for s, v in all_dma_final.items():
                        if seen.get(s, 0) < v:
                            e.wait_ge(sems[s], v)

            handles[eng](body)


class Blk:
    def __init__(self, ntok, tiles, sample, first, last, pidx):
        self.ntok, self.tiles, self.sample, self.first, self.last, self.pidx = ntok, tiles, sample, first, last, pidx


def build_program(depth=DEPTH, npb=NPB):
    nc = bass.Bass("TRN2", target_bir_lowering=False)
    S = Sched()

    def din(name, shape):
        return nc.dram_tensor(name, shape, F32, kind="ExternalInput").ap()

    def dout(name, shape):
        return nc.dram_tensor(name, shape, F32, kind="ExternalOutput").ap()

    x_p = din("x_p", [npb * NB, D])
    x_s = din("x_s", [NS, D])
    st_a = din("st_a", [DEPTH, NSEG, KA - 1, WA])
    st_b = din("st_b", [DEPTH, NSEG, KB - 1, WA])
    norm_g = din("norm_g", [DEPTH, 8, 128])
    w_in = din("w_in", [DEPTH, D, INW])
    a_conv_w = din("a_conv_w", [DEPTH, KA, WA])
    a_conv_b = din("a_conv_b", [DEPTH, 1, WA])
    a_ln_g = din("a_ln_g", [DEPTH, 1, WA])
    a_ln_b = din("a_ln_b", [DEPTH, 1, WA])
    b_conv_w = din("b_conv_w", [DEPTH, KB, WA])
    c_ln_g = din("c_ln_g", [DEPTH, 1, WC])
    c_ln_b = din("c_ln_b", [DEPTH, 1, WC])
    c_ws = din("c_ws", [DEPTH, 4, 128, 128])
    c_bs = din("c_bs", [DEPTH, 4, 128])
    w_out = din("w_out", [DEPTH, D, D])
    final_g = din("final_g", [1, D])

    y_p = dout("y_p", [npb * NB, D])
    y_s = dout("y_s", [NS, D])
    sa_p = dout("sa_p", [DEPTH, KA - 1, WA])
    sb_p = dout("sb_p", [DEPTH, KB - 1, WA])
    sa_s = dout("sa_s", [DEPTH, NSEG, KA - 1, WA])
    sb_s = dout("sb_s", [DEPTH, NSEG, KB - 1, WA])
    sv_s = dout("sv_s", [DEPTH, NSEG, DSEQ, WC])

    w_in_scr = nc.dram_tensor("w_in_scr", [128, 8, INW], BF16).ap()
    w_out_scr = nc.dram_tensor("w_out_scr", [128, 8, D], BF16).ap()
    stack = ExitStack()
    with stack:
        def sb(name, shape, dt=F32):
            return stack.enter_context(nc.sbuf_tensor(name, shape, dt))

        def ps(name, shape, dt=F32):
            return stack.enter_context(nc.psum_tensor(name, shape, dt))

        x_all = sb("x_all", [128, 17, D])
        w_in_sb = sb("w_in_sb", [128, 8, INW], BF16)
        w_out_sb = sb("w_out_sb", [128, 8, D], BF16)
        W4 = sb("W4", [128, 12, 8, 32], BF16)
        A4 = sb("A4", [128, 12, NB + 28], BF16)
        id32 = sb("id32", [128, 32], BF16)
        wcol4 = sb("wcol4", [128, 12, 8])
        colsK = sb("colsK", [128, 3, 4, 8])
        h_tm = sb("h_tm", [128, 2, D], BF16)
        h_fm = sb("h_fm", [128, 8, NB], BF16)
        mix_fm = sb("mix_fm", [128, 8, NB], BF16)
        a_ext = sb("a_ext", [128, 3, NSEG * 34 + 4], BF16)
        z_ext = sb("z_ext", [128, 3, NB + 4], BF16)
        th = sb("th", [128, 2, NB])
        cb = sb("cb", [128, 3, NB])
        cbh = sb("cbh", [128, 3, NB], BF16)
        sq = sb("sq", [128, 3, NB], BF16)
        mean_b = sb("mean_b", [128, NB])
        vb = sb("vb", [128, NB])
        szt = sb("szt", [128, 3, NB])
        bh = sb("bh", [128, 2, NB])
        bb = sb("bb", [128, 2, NB])
        szb = sb("szb", [128, 2, NB])
        ncv = sb("ncv", [128, 2, WC])
        vn_bf = sb("vn_bf", [128, 2, WC], BF16)
        tC = sb("tC", [128, 2, NB])
        szc = sb("szc", [128, 2, NB])
        a32 = sb("a32", [128, 3, NS])
        z32 = sb("z32", [128, 3, 32])
        ident_bf = sb("ident_bf", [128, 128], BF16)
        ident_f = sb("ident_f", [128, 128])
        maskT = sb("maskT", [128, 128])
        maskB = sb("maskB", [64, NSEG, 4])
        ones_bf = sb("ones_bf", [128, 128], BF16)
        wmT = sb("wmT", [128, 4, 128], BF16)
        wblk = sb("wblk", [64, 4, 64], BF16)
        cbias = sb("cbias", [128, 2, 128])
        gC_b = sb("gC_b", [128, WC])
        bC_b = sb("bC_b", [128, WC])
        fg_b = sb("fg_b", [128, D])
        colsA = sb("colsA", [128, 3, 40])
        colsH = sb("colsH", [128, 3, 32])
        g_col = sb("g_col", [128, DEPTH * 8])
        stG_all = sb("stG_all", [DEPTH * 8, 128])
        ss = sb("ss", [128, 2])
        rstd = sb("rstd", [128, 2])
        st6 = sb("st6", [128, 2, 6])
        mv = sb("mv", [128, 2, 2])
        rstdc = sb("rstdc", [128, 2])

        NBANK = 8
        pbank = [ps("pb%d" % i, [128, 512]) for i in range(NBANK)]
        hT_state = {}

        state = {"bank": 0}

        def alloc_bank():
            b = state["bank"]
            state["bank"] = (b + 1) % NBANK
            return b

        def pap(bh_, rows=128, n=NB, p0=0):
            b, hf = bh_
            return pbank[b][p0:p0 + rows, hf * 256:hf * 256 + n]

        def pfull(b, rows=128, n=512):
            return pbank[b][0:rows, 0:n]

        def pk(x):
            return ("psb", x[0] if isinstance(x, tuple) else x)

        ones_f = cb[:, 0, 0:128]
        S.add("pool", lambda e: e.memset(ones_f, 1.0), writes=[("cb", 0)])
        S.add("pool", lambda e: e.memset(ones_bf[:], 1.0), writes=["ones_bf"])
        S.add("pool", lambda e: e.affine_select(out=ident_f[:], in_=ones_f, pattern=[[-1, 128]],
                                                compare_op=ALU.is_equal, fill=0.0, base=0, channel_multiplier=1),
              reads=[("cb", 0)], writes=["ident_f"])
        S.add("pool", lambda e: e.affine_select(out=maskT[:], in_=ones_f, pattern=[[-1, 128]],
                                                compare_op=ALU.is_ge, fill=0.0, base=0, channel_multiplier=1),
              reads=[("cb", 0)], writes=["maskT"])
        S.add("pool", lambda e: e.affine_select(out=maskB[:], in_=ones_f[0:64, 0:64].rearrange("p (g s) -> p g s", s=4),
                                                pattern=[[-4, NSEG], [-1, 4]],
                                                compare_op=ALU.is_ge, fill=0.0, base=0, channel_multiplier=1),
              reads=[("cb", 0)], writes=["maskB"])
        S.add("pool", lambda e: e.affine_select(out=maskB[:], in_=maskB[:], pattern=[[4, NSEG], [0, 4]],
                                                compare_op=ALU.is_ge, fill=0.0, base=3, channel_multiplier=-1),
              reads=["maskB"], writes=["maskB"])
        S.add("pool", lambda e: e.tensor_copy(ident_bf[:], ident_f[:]), reads=["ident_f"], writes=["ident_bf"])
        S.add("pool", lambda e: e.tensor_tensor(out=id32[:], in0=ident_bf[:, 0:32], in1=ident_bf[:, 32:64], op=ALU.add),
              reads=["ident_bf"], writes=["id32"])
        S.add("pool", lambda e: e.tensor_tensor(out=id32[:], in0=id32[:], in1=ident_bf[:, 64:96], op=ALU.add),
              reads=["ident_bf", "id32"], writes=["id32"])
        S.add("pool", lambda e: e.tensor_tensor(out=id32[:], in0=id32[:], in1=ident_bf[:, 96:128], op=ALU.add),
              reads=["ident_bf", "id32"], writes=["id32"])
        S.add("pool", lambda e: e.memset(colsH[:], 0.0), writes=["colsH"])
        S.add("sp", lambda e: e.dma_start(out=stG_all[:], in_=norm_g.rearrange("l k p -> (l k) p")),
              writes=["stG_all"], dma=True)
        S.add("pool", lambda e: e.memset(A4[:], 0.0), writes=[("a4", b) for b in range(12)])
        S.add("pool", lambda e: e.memset(a_ext[:], 0.0), writes=[("aext", 0), ("aext", 1), ("aext", 2)])
        def load_x(j0, j1, q_="sp"):
            for j in range(j0, j1):
                if j < 2 * npb:
                    S.add(q_, lambda e, j=j: e.dma_start(out=x_all[:, j, :], in_=x_p[j * 128:(j + 1) * 128, :]),
                          writes=[("x", j)], dma=True)
            if j1 > 2 * npb:
                S.add(q_, lambda e: e.dma_start(out=x_all[0:NS, 16, :], in_=x_s[:, :]),
                      writes=[("x", 16)], dma=True)

        load_x(0, 4)

        cp_state = {"i": 0}

        def copy_op(out, in_, reads, writes, scale=None, eng=None):
            if eng is None:
                eng = "act" if cp_state["i"] % 2 == 0 else "dve"
                cp_state["i"] += 1
            if eng == "act":
                if scale is None:
                    S.add("act", lambda e: e.activation(out=out, in_=in_, func=AF.Copy), reads=reads, writes=writes)
                else:
                    S.add("act", lambda e: e.activation(out=out, in_=in_, func=AF.Copy, scale=scale),
                          reads=reads, writes=writes)
            else:
                if scale is None:
                    S.add("dve", lambda e: e.tensor_copy(out, in_), reads=reads, writes=writes)
                else:
                    S.add("dve", lambda e: e.tensor_scalar(out, in_, scale, None, ALU.mult),
                          reads=reads, writes=writes)

        def pe_transpose(out, in_, ident, reads, writes):
            S.add("pe", lambda e: e.transpose(out, in_, ident), reads=reads, writes=writes)

        _sg = (alloc_bank(), 0)
        pe_transpose(pap(_sg, 128, DEPTH * 8), stG_all[:], ident_f[0:DEPTH * 8, 0:DEPTH * 8], ["stG_all", "ident_f"],
                     [pk(_sg)])
        copy_op(g_col[:, :], pap(_sg, 128, DEPTH * 8), [pk(_sg)], ["g_col"], eng="act")

        K_TH = [("thb", 0), ("thb", 1)]
        K_CB = [("cb", 0), ("cb", 1), ("cb", 2)]
        K_SZT = [("sztb", 0), ("sztb", 1), ("sztb", 2)]
        K_BH = [("bhb", 0), ("bhb", 1)]
        K_BB = [("bbb", 0), ("bbb", 1)]
        K_SZB = [("szbb", 0), ("szbb", 1)]

        def weight_loads(l, which=("early", "late", "wout")):
            wkeys = []
            if l == 0:
                w_in_v = w_in[l].rearrange("(k p) e -> p k e", p=128)
                w_out_v = w_out[l].rearrange("(k p) e -> p k e", p=128)
            else:
                w_in_v, w_out_v = w_in_scr, w_out_scr
            for pi, (c0, w) in enumerate(PIECES):
                if ("early" if pi < 5 else "late") not in which:
                    continue
                S.add("pool", lambda e, c0=c0, w=w: e.dma_start(out=w_in_sb[:, :, c0:c0 + w], in_=w_in_v[:, :, c0:c0 + w]),
                      reads=(wkeys[-4:-3] if l == 0 else [("wscr", pi)]), writes=[("win", pi)], dma=True)
                wkeys.append(("win", pi))
            if "wout" in which:
                for hf in range(2):
                    S.add("pool", lambda e, hf=hf: e.dma_start(out=w_out_sb[:, :, hf * 512:(hf + 1) * 512],
                                                               in_=w_out_v[:, :, hf * 512:(hf + 1) * 512]),
                          reads=(wkeys[-4:-3] if l == 0 else [("wscr_o", hf)]), writes=[("wout", hf)], dma=True)
                    wkeys.append(("wout", hf))

        def weight_prestage(l):
            w_in_v = w_in[l].rearrange("(k p) e -> p k e", p=128)
            w_out_v = w_out[l].rearrange("(k p) e -> p k e", p=128)
            for pi, (c0, w) in enumerate(PIECES):
                S.add("pool", lambda e, c0=c0, w=w: e.dma_start(out=w_in_scr[:, :, c0:c0 + w],
                                                                in_=w_in_v[:, :, c0:c0 + w]),
                      writes=[("wscr", pi)], dma=True)
            for hf in range(2):
                S.add("pool", lambda e, hf=hf: e.dma_start(out=w_out_scr[:, :, hf * 512:(hf + 1) * 512],
                                                           in_=w_out_v[:, :, hf * 512:(hf + 1) * 512]),
                      writes=[("wscr_o", hf)], dma=True)

        def layer_setup_A(l):
            if l == 0:
                weight_loads(l)
            S.add("sp", lambda e: e.dma_start(out=gC_b[:], in_=bass.AP(c_ln_g.tensor, l * WC, [[0, 128], [1, WC]])),
                  writes=["gC_b"], dma=True)
            S.add("sp", lambda e: e.dma_start(out=bC_b[:], in_=bass.AP(c_ln_b.tensor, l * WC, [[0, 128], [1, WC]])),
                  writes=["bC_b"], dma=True)
            S.add("dve", lambda e: e.memset(a_ext[:, :, 0:KA - 1], 0.0), writes=[("aext", 0), ("aext", 1), ("aext", 2)])
            S.add("dve", lambda e: e.memset(z_ext[:, :, 0:KB - 1], 0.0), writes=[("zext", 0), ("zext", 1), ("zext", 2)])

        def layer_setup_Bdma(l, q_="sp"):
            thf = cb[:].rearrange("p a b -> p (a b)")
            sztf = szt[:].rearrange("p a b -> p (a b)")
            stP = thf[0:37, 0:WA]
            srcs = [(a_conv_w[l], 0, KA), (a_conv_b[l], KA, 1), (a_ln_g[l], KA + 1, 1), (a_ln_b[l], KA + 2, 1),
                    (b_conv_w[l], KA + 3, KB)]
            for i_, (src, r0, nr) in enumerate(srcs):
                S.add(q_, lambda e, src=src, r0=r0, nr=nr: e.dma_start(out=thf[r0:r0 + nr, 0:WA], in_=src),
                      writes=(K_CB if i_ == 0 else []) + [("stP", i_)], dma=True)
            wsn = sztf[:, 0:512].rearrange("p (h s) -> p h s", h=4)
            S.add(q_, lambda e: e.dma_start(out=wsn, in_=c_ws[l].rearrange("h t s -> t h s")),
                  writes=K_SZT, dma=True)
            wrc = vb[0:64, 128:144].rearrange("p (h s) -> p h s", h=4)
            for g in range(NSEG):
                src = bass.AP(c_ws.tensor, l * 4 * 128 * 128, [[128, 4], [128 * 128, 4], [1, 4]])
                S.add(q_, lambda e, g=g, src=src: e.dma_start(out=wrc[4 * g:4 * g + 4], in_=src),
                      writes=(["vb"] if g == 0 else []) + [("wrc", g)], dma=True)
            for h in range(4):
                src = bass.AP(c_bs.tensor, (l * 4 + h) * 128, [[0, 64], [1, 128]])
                S.add(q_, lambda e, h=h, src=src: e.dma_start(out=cbias[(h % 2) * 64:(h % 2) * 64 + 64, h // 2, :],
                                                              in_=src), writes=[("cbias", h)], dma=True)
            S.add(q_, lambda e: e.dma_start(out=sa_s[l, :, 0:KA - 1 - DSEQ, :], in_=st_a[l, :, DSEQ:KA - 1, :]),
                  writes=[("sa_s_copy", l)], dma=True)

        def layer_setup_Bparams(l):
            thf = cb[:].rearrange("p a b -> p (a b)")
            stP = thf[0:37, 0:WA]
            for c in range(3):
                s = (alloc_bank(), 0)
                pe_transpose(pap(s, 128, 37), stP[:, c * 128:(c + 1) * 128], ident_f[0:37, 0:37],
                             K_CB + [("stP", i_) for i_ in range(5)] + ["ident_f"], [pk(s)])
                copy_op(colsA[:, c, 0:37], pap(s, 128, 37), [pk(s)], ["colsA"], eng="act")
            S.add("dve", lambda e: e.tensor_scalar(colsH[:, :, 0:KA], colsA[:, :, 0:KA], 0.5, None, ALU.mult),
                  reads=["colsA"], writes=["colsH"])
            S.add("dve", lambda e: e.tensor_copy(colsK[:], colsH[:, :, 0:32].rearrange("p c (m j) -> p c j m", j=4)),
                  reads=["colsH"], writes=["colsK"])
            for q in range(4):
                for j in range(4):
                    S.add("sp", lambda e, q=q, j=j: e.dma_start(
                        out=wcol4[32 * j:32 * j + 32, :, :].rearrange("p (c q) m -> p c q m", q=4)[:, :, q, :],
                        in_=colsK[32 * q:32 * q + 32, :, j, :]),
                        reads=["colsK"], writes=[("wcol4", j, q)], dma=True)
            sztf_ = szt[:].rearrange("p a b -> p (a b)")
            wsn = sztf_[:, 0:512].rearrange("p (h s) -> p h s", h=4)
            wrc = vb[0:64, 128:144].rearrange("p (h s) -> p h s", h=4)
            wr = mean_b[0:64, :].rearrange("p (h g s) -> p h g s", h=4, g=NSEG)
            S.add("dve", lambda e: e.tensor_tensor(out=wsn, in0=wsn, in1=maskT[:].unsqueeze(1).broadcast_to([128, 4, 128]),
                                                   op=ALU.mult),
                  reads=K_SZT + ["maskT"], writes=K_SZT)
            S.add("dve", lambda e: e.tensor_tensor(out=wr, in0=wrc.unsqueeze(2).broadcast_to([64, 4, NSEG, 4]),
                                                   in1=maskB[:].unsqueeze(1).broadcast_to([64, 4, NSEG, 4]), op=ALU.mult),
                  reads=["vb", "maskB"] + [("wrc", g) for g in range(NSEG)], writes=["mean_b"])
            def w4_build():
                WC4 = [("wcol4", j, q) for j in range(4) for q in range(4)]
                S.add("dve", lambda e: e.tensor_tensor(
                    out=W4[:].rearrange("p b m c -> p (b m) c"),
                    in0=id32[:].unsqueeze(1).broadcast_to([128, 96, 32]),
                    in1=wcol4[:].rearrange("p b m -> p (b m)").unsqueeze(2).broadcast_to([128, 96, 32]), op=ALU.mult),
                    reads=WC4 + ["id32"], writes=["w4"])
            return w4_build

        def layer_setup_Bcomp(l):
            thf = cb[:].rearrange("p a b -> p (a b)")
            sztf = szt[:].rearrange("p a b -> p (a b)")
            stP = thf[0:37, 0:WA]
            wsn = sztf[:, 0:512].rearrange("p (h s) -> p h s", h=4)
            wrc = vb[0:64, 128:144].rearrange("p (h s) -> p h s", h=4)
            for h in range(4):
                s = (alloc_bank(), 0)
                pe_transpose(pap(s, 128, 128), wsn[:, h, :], ident_f[:], K_SZT + ["ident_f"], [pk(s)])
                copy_op(wmT[:, h, :], pap(s, 128, 128), [pk(s)], ["wmT"])
            wr = mean_b[0:64, :].rearrange("p (h g s) -> p h g s", h=4, g=NSEG)
            wr3 = mean_b[0:64, :].rearrange("p (h q) -> p h q", h=4)
            for h in range(4):
                s = (alloc_bank(), 0)
                pe_transpose(pap(s, 64, 64), wr3[:, h, :], ident_f[0:64, 0:64], ["mean_b", "ident_f"], [pk(s)])
                copy_op(wblk[:, h, :], pap(s, 64, 64), [pk(s)], ["wblk"])

        def _hist_stages():
            return [(szb[:].rearrange("p a b -> p (a b)"), K_SZB),
                    (tC[:].rearrange("p a b -> p (a b)"), [("tC", 0), ("tC", 1)]),
                    (szc[:].rearrange("p a b -> p (a b)"), [("szc", 0), ("szc", 1)]),
                    (bb[:].rearrange("p a b -> p (a b)"), K_BB)]

        def sample_hist_a_dma(l):
            for q, (stg, key) in enumerate(_hist_stages()):
                src = st_a[l, 4 * q:4 * q + 4].rearrange("g r c -> (g r) c")
                S.add("sp", lambda e, stg=stg, src=src: e.dma_start(out=stg[0:120, 0:WA], in_=src),
                      writes=key, dma=True)

        def sample_hist_a_pe(l):
            aS = a_ext[:, :, 0:NSEG * 34].rearrange("p c (g r) -> p c g r", r=34)
            for q, (stg, key) in enumerate(_hist_stages()):
                for c in range(3):
                    s = (alloc_bank(), 0)
                    pe_transpose(pap(s, 128, 120), stg[0:120, c * 128:(c + 1) * 128], ident_f[0:120, 0:120],
                                 key + ["ident_f"], [pk(s)])
                    copy_op(aS[:, c, 4 * q:4 * q + 4, 0:KA - 1],
                            pap(s, 128, 120).rearrange("p (g r) -> p g r", r=KA - 1), [pk(s)], [("aext", c)],
                            scale=2.0)

        K_NCV = [("ncv", 0), ("ncv", 1)]

        def sample_hist_z_dma(l):
            ncvf = ncv[:].rearrange("p a b -> p (a b)")
            S.add("sp", lambda e: e.dma_start(out=ncvf[0:32, 0:WA], in_=st_b[l].rearrange("g r c -> (g r) c")),
                  writes=K_NCV, dma=True)

        def sample_hist_z(l):
            ncvf = ncv[:].rearrange("p a b -> p (a b)")
            zS = z_ext[:, :, 0:NSEG * 6].rearrange("p c (g r) -> p c g r", r=6)
            for c in range(3):
                s = (alloc_bank(), 0)
                pe_transpose(pap(s, 128, 32), ncvf[0:32, c * 128:(c + 1) * 128], ident_f[0:32, 0:32],
                             K_NCV + ["ident_f"], [pk(s)])
                copy_op(zS[:, c, :, 0:KB - 1], pap(s, 128, 32).rearrange("p (g r) -> p g r", r=KB - 1),
                        [pk(s)], [("zext", c)])

        def rstd_from_sumsq(rows, nt):
            S.add("act", lambda e: e.activation(out=rstd[0:rows, 0:nt], in_=ss[0:rows, 0:nt], func=AF.Sqrt,
                                                bias=eps_col[0:rows, :], scale=1.0 / D),
                  reads=["ss", "eps"], writes=["rstd"])
            S.add("dve", lambda e: e.reciprocal(rstd[0:rows, 0:nt], rstd[0:rows, 0:nt]),
                  reads=["rstd"], writes=["rstd"])

        def H_elem(blk):
            nt = len(blk.tiles)
            rows = blk.tiles[0][1]
            for j, (slot, r) in enumerate(blk.tiles):
                S.add("act", lambda e, j=j, slot=slot, r=r: e.activation(
                    out=h_tm[0:r, j, :], in_=x_all[0:r, slot, :], func=AF.Square, accum_out=ss[0:r, j:j + 1]),
                    reads=[("x", slot)], writes=[("htm", j), "ss"], multi=True)
            rstd_from_sumsq(rows, nt)
            for j, (slot, r) in enumerate(blk.tiles):
                S.add("dve", lambda e, j=j, slot=slot, r=r: e.tensor_scalar(
                    h_tm[0:r, j, :], x_all[0:r, slot, :], rstd[0:r, j:j + 1], None, ALU.mult),
                    reads=[("x", slot), "rstd"], writes=[("htm", j)])

        def hT_view(bank):
            return pbank[bank][:].bitcast(BF16).rearrange("p (k t) -> p k t", t=NB)

        def H_pe(blk):
            banks = (alloc_bank(), alloc_bank())
            hT_state["banks"] = banks
            for j, (slot, r) in enumerate(blk.tiles):
                for k in range(8):
                    pe_transpose(hT_view(banks[k // 4])[:, k % 4, j * 128:j * 128 + r], h_tm[0:r, j, k * 128:(k + 1) * 128],
                                 ident_bf[0:r, 0:r], [("htm", j), "ident_bf"], [pk(banks[k // 4])])

        def H_evac(blk, l):
            n = blk.ntok
            banks = hT_state["banks"]
            for k in range(8):
                if k < 4:
                    S.add("act", lambda e, k=k: e.activation(out=h_fm[:, k, 0:n], in_=hT_view(banks[0])[:, k % 4, 0:n],
                                                            func=AF.Copy, scale=g_col[:, l * 8 + k:l * 8 + k + 1]),
                          reads=[pk(banks[0]), "g_col"], writes=[("hfm", k)])
                else:
                    S.add("dve", lambda e, k=k: e.tensor_scalar(h_fm[:, k, 0:n], hT_view(banks[1])[:, k % 4, 0:n],
                                                               g_col[:, l * 8 + k:l * 8 + k + 1], None, ALU.mult),
                          reads=[pk(banks[1]), "g_col"], writes=[("hfm", k)])

        HFM = [("hfm", k) for k in range(8)]

        def proj_fm(col0, n, s):
            pi = piece_of(col0)
            for k in range(8):
                S.add("pe", lambda e, k=k, s=s: e.matmul(pap(s, 128, n), w_in_sb[:, k, col0:col0 + 128],
                                                        h_fm[:, k, 0:n], start=(k == 0), stop=(k == 7)),
                      reads=[("win", pi)] + HFM, writes=[pk(s)])
            return s

        def main_early(l, blk):
            n = blk.ntok
            smp = blk.sample
            if smp:
                aS = a_ext[:, :, 0:NSEG * 34].rearrange("p c (g r) -> p c g r", r=34)
                zS = z_ext[:, :, 0:NSEG * 6].rearrange("p c (g r) -> p c g r", r=6)

                def v3(ap2):
                    return ap2.rearrange("p (g i) -> p g i", i=DSEQ)

                def awin(c, k):
                    return aS[:, c, :, k:k + DSEQ]

                def zwin(c, k):
                    return zS[:, c, :, k:k + DSEQ]
            else:
                def v3(ap2):
                    return ap2

                def awin(c, k):
                    return a_ext[:, c, k:k + n]

                def zwin(c, k):
                    return z_ext[:, c, k:k + n]
            nt = len(blk.tiles)
            rows = blk.tiles[0][1]

            for c in range(3):
                bk = alloc_bank()
                sv = proj_fm(C_AV + c * 128, n, (bk, 0))
                sg = proj_fm(C_AG + c * 128, n, (bk, 1))
                tb = c % 2
                S.add("act", lambda e, sg=sg, tb=tb: e.activation(out=th[:, tb, 0:n], in_=pap(sg, 128, n),
                                                                  func=AF.Tanh, scale=0.5),
                      reads=[pk(sg)], writes=[("thb", tb)])
                S.add("dve", lambda e, sv=sv, tb=tb, c=c: e.scalar_tensor_tensor(
                    out=awin(c, KA - 1), in0=v3(th[:, tb, 0:n]), scalar=1.0, in1=v3(pap(sv, 128, n)),
                    op0=ALU.add, op1=ALU.mult),
                    reads=[("thb", tb), pk(sv)], writes=[("aext", c)])
                if blk.last or smp:
                    c0, nc_ = (0, NS) if smp else (n - (KA - 1), KA - 1)
                    S.add("dve", lambda e, sv=sv, tb=tb, c=c, c0=c0, nc_=nc_: e.scalar_tensor_tensor(
                        out=a32[:, c, 0:nc_], in0=th[:, tb, c0:c0 + nc_], scalar=1.0,
                        in1=pap(sv, 128, n)[:, c0:c0 + nc_], op0=ALU.add, op1=ALU.mult),
                        reads=[("thb", tb), pk(sv)], writes=[("a32", c)])
            for c in range(3):
                bk = alloc_bank()
                sh = proj_fm(C_BH + c * 128, n, (bk, 0))
                scc = proj_fm(C_BC + c * 128, n, (bk, 1))
                tb = c % 2
                S.add("act", lambda e, sh=sh, tb=tb: e.activation(out=bh[:, tb, 0:n], in_=pap(sh, 128, n), func=AF.Copy),
                      reads=[pk(sh)], writes=[("bhb", tb)])
                S.add("dve", lambda e, scc=scc, tb=tb, c=c: e.tensor_tensor(
                    out=zwin(c, KB - 1), in0=v3(pap(scc, 128, n)), in1=v3(bh[:, tb, 0:n]), op=ALU.mult),
                    reads=[("bhb", tb), pk(scc)], writes=[("zext", c)])
                if blk.last:
                    S.add("dve", lambda e, scc=scc, tb=tb, c=c: e.tensor_tensor(
                        out=z32[:, c, 0:2], in0=pap(scc, 128, n)[:, n - 2:n], in1=bh[:, tb, n - 2:n], op=ALU.mult),
                        reads=[("bhb", tb), pk(scc)], writes=[("z32", c)])
                if smp:
                    S.add("dve", lambda e, scc=scc, tb=tb, c=c: e.tensor_tensor(
                        out=z32[:, c, 0:32].rearrange("p (g i) -> p g i", i=2),
                        in0=v3(pap(scc, 128, n))[:, :, 2:4], in1=v3(bh[:, tb, 0:n])[:, :, 2:4], op=ALU.mult),
                        reads=[("bhb", tb), pk(scc)], writes=[("z32", c)])
            if smp:
                pass
            else:
                ncol = n + 28
                nmv = ncol + 4
                for c in range(3):
                    for q in range(4):
                        b = 4 * c + q
                        bk = alloc_bank()
                        for j in range(4):
                            S.add("pe", lambda e, c=c, q=q, j=j, bk=bk: e.matmul(
                                pbank[bk][32 * j:32 * j + 32, 3 - j:3 - j + nmv], ident_bf[:, 32 * q:32 * q + 32],
                                a_ext[:, c, 0:nmv], start=True, stop=True, tile_position=(0, 32 * j)),
                                reads=[("aext", c), "ident_bf"], writes=[pk(bk)])
                        copy_op(A4[:, b, 0:ncol], pbank[bk][:, 3:3 + ncol], [pk(bk)], [("a4", b)])
            scv = []
            bkv = alloc_bank()
            for j, (slot, r) in enumerate(blk.tiles):
                s = (bkv, j)
                scv.append(s)
                pi = piece_of(C_CV)
                for k in range(8):
                    S.add("pe", lambda e, k=k, s=s, j=j, r=r: e.matmul(
                        pap(s, r, WC), h_fm[:, k, j * 128:j * 128 + r], w_in_sb[:, k, C_CV:C_CV + WC],
                        start=(k == 0), stop=(k == 7)),
                        reads=[("win", pi)] + HFM, writes=[pk(s)])
            for j, (slot, r) in enumerate(blk.tiles):
                s = scv[j]
                S.add("dve", lambda e, s=s, j=j, r=r: e.bn_stats(st6[0:r, j, :], pap(s, r, WC)),
                      reads=[pk(s)], writes=["st6"])
                S.add("dve", lambda e, j=j, r=r: e.bn_aggr(mv[0:r, j, :], st6[0:r, j, :]),
                      reads=["st6"], writes=["mv"])
            S.add("act", lambda e: e.activation(out=rstdc[0:rows, 0:nt], in_=mv[0:rows, 0:nt, 1], func=AF.Sqrt,
                                                bias=eps_col[0:rows, :], scale=1.0),
                  reads=["mv", "eps"], writes=["rstdc"])
            S.add("dve", lambda e: e.reciprocal(rstdc[0:rows, 0:nt], rstdc[0:rows, 0:nt]),
                  reads=["rstdc"], writes=["rstdc"])
            for j, (slot, r) in enumerate(blk.tiles):
                s = scv[j]
                S.add("dve", lambda e, s=s, j=j, r=r: e.tensor_scalar(
                    ncv[0:r, j, :], pap(s, r, WC), mv[0:r, j, 0:1], rstdc[0:r, j:j + 1], ALU.subtract, ALU.mult),
                    reads=[pk(s), "mv", "rstdc"], writes=[("ncv", j)])
                S.add("dve", lambda e, j=j, r=r: e.tensor_tensor(out=ncv[0:r, j, :], in0=ncv[0:r, j, :],
                                                                  in1=gC_b[0:r, :], op=ALU.mult),
                      reads=[("ncv", j), "gC_b"], writes=[("ncv", j)])
                if smp:
                    S.add("dve", lambda e, j=j, r=r: e.tensor_tensor(out=ncv[0:r, j, :], in0=ncv[0:r, j, :],
                                                                      in1=bC_b[0:r, :], op=ALU.add),
                          reads=[("ncv", j), "bC_b"], writes=[("ncv", j)])
                    S.add("sp", lambda e, r=r: e.dma_start(out=sv_s[l].rearrange("g i c -> (g i) c"),
                                                           in_=ncv[0:r, 0, :]),
                          reads=[("ncv", 0)], writes=[("sv_s", l)], dma=True)
                    S.add("act", lambda e, j=j, r=r: e.activation(out=vn_bf[0:r, j, :], in_=ncv[0:r, j, :],
                                                                  func=AF.Copy),
                          reads=[("ncv", j)], writes=[("vnbf", j)])
                else:
                    S.add("dve", lambda e, j=j, r=r: e.tensor_tensor(out=vn_bf[0:r, j, :], in0=ncv[0:r, j, :],
                                                                      in1=bC_b[0:r, :], op=ALU.add),
                          reads=[("ncv", j), "bC_b"], writes=[("vnbf", j)])

        def main_late(l, blk, next_blk, pre_conv=None, after_proj=None, h_next=None):
            n = blk.ntok
            smp = blk.sample
            if smp:
                aS = a_ext[:, :, 0:NSEG * 34].rearrange("p c (g r) -> p c g r", r=34)
                zS = z_ext[:, :, 0:NSEG * 6].rearrange("p c (g r) -> p c g r", r=6)

                def v3(ap2):
                    return ap2.rearrange("p (g i) -> p g i", i=DSEQ)

                def awin(c, k):
                    return aS[:, c, :, k:k + DSEQ]

                def zwin(c, k):
                    return zS[:, c, :, k:k + DSEQ]
            else:
                def v3(ap2):
                    return ap2

                def awin(c, k):
                    return a_ext[:, c, k:k + n]

                def zwin(c, k):
                    return z_ext[:, c, k:k + n]
            nt = len(blk.tiles)
            rows = blk.tiles[0][1]
            hn_blk, hn_l = (next_blk, l) if next_blk is not None else (h_next if h_next is not None else (None, l))
            if hn_blk is not None:
                H_elem(hn_blk)
            for c in range(3):
                bk = alloc_bank()
                sbb = proj_fm(C_BB + c * 128, n, (bk, 0))
                szs = proj_fm(C_BZ + c * 128, n, (bk, 1))
                tb = c % 2
                S.add("act", lambda e, szs=szs, tb=tb: e.activation(out=szb[:, tb, 0:n], in_=pap(szs, 128, n),
                                                                    func=AF.Silu),
                      reads=[pk(szs)], writes=[("szbb", tb)])
                for k in range(KB):
                    if k == 0:
                        S.add("dve", lambda e, c=c, k=k, tb=tb: e.tensor_scalar(
                            v3(bb[:, tb, 0:n]), zwin(c, k), colsA[:, c, KA + 3 + k:KA + 4 + k], None, ALU.mult),
                            reads=[("zext", c), "colsA"], writes=[("bbb", tb)])
                    else:
                        S.add("dve", lambda e, c=c, k=k, tb=tb: e.scalar_tensor_tensor(
                            out=v3(bb[:, tb, 0:n]), in0=zwin(c, k), scalar=colsA[:, c, KA + 3 + k:KA + 4 + k],
                            in1=v3(bb[:, tb, 0:n]), op0=ALU.mult, op1=ALU.add),
                            reads=[("zext", c), "colsA", ("bbb", tb)], writes=[("bbb", tb)])
                if not smp:
                    S.add("dve", lambda e, c=c: e.tensor_copy(z_ext[:, c, 0:KB - 1], z_ext[:, c, n:n + KB - 1]),
                          reads=[("zext", c)], writes=[("zext", c)])
                S.add("dve", lambda e, sbb=sbb, tb=tb: e.tensor_tensor(out=bb[:, tb, 0:n], in0=pap(sbb, 128, n),
                                                                       in1=bb[:, tb, 0:n], op=ALU.mult),
                      reads=[pk(sbb), ("bbb", tb)], writes=[("bbb", tb)])
                S.add("dve", lambda e, c=c, tb=tb: e.tensor_tensor(out=mix_fm[:, 3 + c, 0:n], in0=bb[:, tb, 0:n],
                                                                   in1=szb[:, tb, 0:n], op=ALU.mult),
                      reads=[("bbb", tb), ("szbb", tb)], writes=[("mix", 3 + c)])
            bkz = alloc_bank()
            cslots = []
            for hc in range(2):
                bk = alloc_bank()
                scm = (bk, 0)
                for j, (slot, r) in enumerate(blk.tiles):
                    for hh in range(2):
                        h = 2 * hc + hh
                        rhs = wblk[:, h, :] if smp else wmT[:, h, :]
                        S.add("pe", lambda e, j=j, r=r, hh=hh, h=h, rhs=rhs, scm=scm: e.matmul(
                            pbank[scm[0]][64 * hh:64 * hh + 64, j * 128:j * 128 + r],
                            vn_bf[0:r, j, h * 64:(h + 1) * 64], rhs, start=True, stop=True),
                            reads=[("vnbf", j), "wblk" if smp else "wmT"], writes=[pk(scm)])
                su = proj_fm(C_CU + hc * 128, n, (bk, 1))
                cslots.append((scm, su))
            szzs = [proj_fm(C_CZ + hc * 128, n, (bkz, hc)) for hc in range(2)]
            for hc in range(2):
                scm, su = cslots[hc]
                szz = szzs[hc]
                if smp:
                    cbv = cbias[:, hc, 0:DSEQ].unsqueeze(1).broadcast_to([128, NSEG, DSEQ])
                    pv = v3(pap(scm, 128, n))
                    tv = v3(tC[:, hc, 0:n])
                else:
                    cbv = cbias[:, hc, :].unsqueeze(1).broadcast_to([128, nt, 128])
                    pv = pap(scm, 128, n).rearrange("p (j t) -> p j t", t=128)
                    tv = tC[:, hc, 0:n].rearrange("p (j t) -> p j t", t=128)
                S.add("dve", lambda e, pv=pv, tv=tv, cbv=cbv: e.tensor_tensor(out=tv, in0=pv, in1=cbv, op=ALU.add),
                      reads=[pk(scm)] + [("cbias", h_) for h_ in range(4)], writes=[("tC", hc)])
                S.add("dve", lambda e, su=su, hc=hc: e.tensor_tensor(out=tC[:, hc, 0:n], in0=pap(su, 128, n),
                                                                     in1=tC[:, hc, 0:n], op=ALU.mult),
                      reads=[pk(su), ("tC", hc)], writes=[("tC", hc)])
                S.add("act", lambda e, szz=szz, hc=hc: e.activation(out=szc[:, hc, 0:n], in_=pap(szz, 128, n),
                                                                    func=AF.Silu),
                      reads=[pk(szz)], writes=[("szc", hc)])
                S.add("dve", lambda e, hc=hc: e.tensor_tensor(out=mix_fm[:, 6 + hc, 0:n], in0=tC[:, hc, 0:n],
                                                              in1=szc[:, hc, 0:n], op=ALU.mult),
                      reads=[("tC", hc), ("szc", hc)], writes=[("mix", 6 + hc)])
            bkaz = [alloc_bank(), alloc_bank()]
            szs_ = [proj_fm(C_AZ + c * 128, n, (bkaz[c // 2], c % 2)) for c in range(3)]
            for c in range(3):
                S.add("act", lambda e, c=c: e.activation(out=szt[:, c, 0:n], in_=pap(szs_[c], 128, n), func=AF.Silu),
                      reads=[pk(szs_[c])], writes=[("sztb", c)])
            if hn_blk is not None:
                H_pe(hn_blk)
                H_evac(hn_blk, hn_l)
            if after_proj is not None:
                after_proj()
            if pre_conv is not None:
                pre_conv()
            if smp:
                n_h = 34 * 7 + DSEQ
                ncol_s = n_h + 28
                nmv_s = ncol_s + 4
                for hf in range(2):
                    base = 272 * hf
                    for c in range(3):
                        for q in range(4):
                            b = 4 * c + q
                            bk = alloc_bank()
                            for j in range(4):
                                S.add("pe", lambda e, c=c, q=q, j=j, bk=bk, base=base: e.matmul(
                                    pbank[bk][32 * j:32 * j + 32, 3 - j:3 - j + nmv_s], ident_bf[:, 32 * q:32 * q + 32],
                                    a_ext[:, c, base:base + nmv_s], start=True, stop=True, tile_position=(0, 32 * j)),
                                    reads=[("aext", c), "ident_bf"], writes=[pk(bk)])
                            copy_op(A4[:, b, 0:ncol_s], pbank[bk][:, 3:3 + ncol_s], [pk(bk)], [("a4", b)])
                    for c in range(3):
                        bk = alloc_bank()
                        for m in range(8):
                            for q in range(4):
                                b = 4 * c + q
                                S.add("pe", lambda e, bk=bk, m=m, q=q, b=b: e.matmul(
                                    pbank[bk][32 * q:32 * q + 32, 0:n_h], W4[:, b, m, :], A4[:, b, 4 * m:4 * m + n_h],
                                    start=(m == 0), stop=(m == 7), tile_position=(0, 32 * q)),
                                    reads=["w4", ("a4", b)], writes=[pk(bk)])
                        src = pbank[bk][:, 0:272].rearrange("p (g r) -> p g r", r=34)[:, :, 0:DSEQ]
                        for dst_t, key, fn_ in ((cbh, "cbh", AF.Identity), (sq, "sq", AF.Square), (cb, "cb", AF.Identity)):
                            dst = dst_t[:, c, hf * 32:(hf + 1) * 32].rearrange("p (g i) -> p g i", i=DSEQ)
                            S.add("act", lambda e, dst=dst, src=src, fn_=fn_, c=c: e.activation(
                                out=dst, in_=src, func=fn_, bias=colsA[:, c, KA:KA + 1]),
                                reads=[pk(bk), "colsA", (key, c)], writes=[(key, c)])
            for c in (range(3) if not smp else ()):
                bk = alloc_bank()
                for m in range(8):
                    for q in range(4):
                        b = 4 * c + q
                        S.add("pe", lambda e, bk=bk, m=m, q=q, b=b: e.matmul(
                            pbank[bk][32 * q:32 * q + 32, 0:n], W4[:, b, m, :], A4[:, b, 4 * m:4 * m + n],
                            start=(m == 0), stop=(m == 7), tile_position=(0, 32 * q)),
                            reads=["w4", ("a4", b)], writes=[pk(bk)])
                if not blk.last:
                    S.add("dve", lambda e, c=c: e.tensor_copy(a_ext[:, c, 0:KA - 1], a_ext[:, c, n:n + KA - 1]),
                          reads=[("aext", c)], writes=[("aext", c)])
                S.add("act", lambda e, c=c, bk=bk: e.activation(out=cbh[:, c, 0:n], in_=pbank[bk][:, 0:n],
                                                                func=AF.Identity, bias=colsA[:, c, KA:KA + 1]),
                      reads=[pk(bk), "colsA"], writes=[("cbh", c)])
                S.add("act", lambda e, c=c, bk=bk: e.activation(out=sq[:, c, 0:n], in_=pbank[bk][:, 0:n],
                                                                func=AF.Square, bias=colsA[:, c, KA:KA + 1]),
                      reads=[pk(bk), "colsA"], writes=[("sq", c)])
                S.add("dve", lambda e, c=c, bk=bk: e.tensor_scalar(cb[:, c, 0:n], pbank[bk][:, 0:n],
                                                                   colsA[:, c, KA:KA + 1], None, ALU.add),
                      reads=[pk(bk), "colsA"], writes=[("cb", c)])
            if blk.last:
                so = alloc_bank()
                for c in range(3):
                    pe_transpose(pfull(so, KA - 1, 512)[:, c * 128:(c + 1) * 128], a32[:, c, 0:KA - 1], ident_f[:],
                                 [("a32", c), "ident_f"], [pk(so)])
                stg = tC[:].rearrange("p a b -> p (a b)")
                copy_op(stg[0:KA - 1, 0:WA], pfull(so, KA - 1, WA), [pk(so)], [("tC", 0), ("tC", 1)],
                        scale=0.5, eng="act")
                S.add("sp", lambda e: e.dma_start(out=sa_p[l], in_=stg[0:KA - 1, 0:WA]),
                      reads=[("tC", 0), ("tC", 1)], writes=[("sa_p", l)], dma=True)
                so2 = alloc_bank()
                for c in range(3):
                    pe_transpose(pfull(so2, 2, 512)[:, c * 128:(c + 1) * 128], z32[:, c, 0:2], ident_f[:],
                                 [("z32", c), "ident_f"], [pk(so2)])
                stg2 = szc[:].rearrange("p a b -> p (a b)")
                copy_op(stg2[0:2, 0:WA], pfull(so2, 2, WA), [pk(so2)], [("szc", 0), ("szc", 1)], eng="act")
                S.add("sp", lambda e: e.dma_start(out=sb_p[l], in_=stg2[0:2, 0:WA]),
                      reads=[("szc", 0), ("szc", 1)], writes=[("sb_p", l)], dma=True)
            if smp:
                stg = tC[:].rearrange("p a b -> p (a b)")
                for i in range(DSEQ):
                    so = alloc_bank()
                    for c in range(3):
                        pe_transpose(pfull(so, NSEG, 512)[:, c * 128:(c + 1) * 128],
                                     a32[:, c, 0:NS].rearrange("p (g i) -> p g i", i=DSEQ)[:, :, i], ident_f[:],
                                     [("a32", c), "ident_f"], [pk(so)])
                    copy_op(stg[0:NSEG, 0:WA], pfull(so, NSEG, WA), [pk(so)], [("tC", 0), ("tC", 1)],
                            scale=0.5, eng="act")
                    S.add("sp", lambda e, i=i: e.dma_start(out=sa_s[l, :, KA - 1 - DSEQ + i, :], in_=stg[0:NSEG, 0:WA]),
                          reads=[("tC", 0), ("tC", 1)], writes=[("sa_s", l, i)], dma=True)
                so2 = alloc_bank()
                for c in range(3):
                    pe_transpose(pfull(so2, 32, 512)[:, c * 128:(c + 1) * 128], z32[:, c, 0:32], ident_f[:],
                                 [("z32", c), "ident_f"], [pk(so2)])
                stg2 = szc[:].rearrange("p a b -> p (a b)")
                copy_op(stg2[0:32, 0:WA], pfull(so2, 32, WA), [pk(so2)], [("szc", 0), ("szc", 1)], eng="act")
                S.add("sp", lambda e: e.dma_start(out=sb_s[l].rearrange("g r c -> (g r) c"), in_=stg2[0:32, 0:WA]),
                      reads=[("szc", 0), ("szc", 1)], writes=[("sb_s", l)], dma=True)
            bks = alloc_bank()
            s1 = (bks, 0)
            for c in range(3):
                S.add("pe", lambda e, c=c: e.matmul(pap(s1, 128, n), ones_bf[:], cbh[:, c, 0:n],
                                                    start=(c == 0), stop=(c == 2)),
                      reads=["ones_bf", ("cbh", c)], writes=[pk(s1)])
            s2 = (bks, 1)
            for c in range(3):
                S.add("pe", lambda e, c=c: e.matmul(pap(s2, 128, n), ones_bf[:], sq[:, c, 0:n],
                                                    start=(c == 0), stop=(c == 2)),
                      reads=["ones_bf", ("sq", c)], writes=[pk(s2)])
            S.add("act", lambda e: e.activation(out=mean_b[:, 0:n], in_=pap(s1, 128, n), func=AF.Copy, scale=1.0 / WA),
                  reads=[pk(s1)], writes=["mean_b"])
            S.add("dve", lambda e: e.tensor_tensor(out=vb[:, 0:n], in0=mean_b[:, 0:n], in1=mean_b[:, 0:n], op=ALU.mult),
                  reads=["mean_b"], writes=["vb"])
            S.add("dve", lambda e: e.scalar_tensor_tensor(out=vb[:, 0:n], in0=pap(s2, 128, n), scalar=1.0 / WA,
                                                          in1=vb[:, 0:n], op0=ALU.mult, op1=ALU.subtract),
                  reads=[pk(s2), "vb"], writes=["vb"])
            S.add("act", lambda e: e.activation(out=vb[:, 0:n], in_=vb[:, 0:n], func=AF.Sqrt, bias=eps_col[:, :],
                                                scale=1.0),
                  reads=["vb", "eps"], writes=["vb"])
            S.add("dve", lambda e: e.reciprocal(vb[:, 0:n], vb[:, 0:n]), reads=["vb"], writes=["vb"])
            for c in range(3):
                S.add("dve", lambda e, c=c: e.tensor_tensor(out=cb[:, c, 0:n], in0=cb[:, c, 0:n], in1=mean_b[:, 0:n],
                                                            op=ALU.subtract),
                      reads=[("cb", c), "mean_b"], writes=[("cb", c)])
                S.add("dve", lambda e, c=c: e.tensor_tensor(out=cb[:, c, 0:n], in0=cb[:, c, 0:n], in1=vb[:, 0:n],
                                                            op=ALU.mult),
                      reads=[("cb", c), "vb"], writes=[("cb", c)])
                S.add("act", lambda e, c=c: e.activation(out=cb[:, c, 0:n], in_=cb[:, c, 0:n], func=AF.Silu,
                                                         bias=colsA[:, c, KA + 2:KA + 3], scale=colsA[:, c, KA + 1:KA + 2]),
                      reads=[("cb", c), "colsA"], writes=[("cb", c)])
                S.add("dve", lambda e, c=c: e.tensor_tensor(out=mix_fm[:, c, 0:n], in0=cb[:, c, 0:n],
                                                            in1=szt[:, c, 0:n], op=ALU.mult),
                      reads=[("cb", c), ("sztb", c)], writes=[("mix", c)])
        MIX = [("mix", k) for k in range(8)]

        def O_stage(l, blk):
            for j, (slot, r) in enumerate(blk.tiles):
                for hf in range(2):
                    so = alloc_bank()
                    for k in range(8):
                        S.add("pe", lambda e, k=k, so=so, j=j, r=r, hf=hf: e.matmul(
                            pfull(so, r, 512), mix_fm[:, k, j * 128:j * 128 + r],
                            w_out_sb[:, k, hf * 512:(hf + 1) * 512], start=(k == 0), stop=(k == 7)),
                            reads=MIX + [("wout", hf)], writes=[pk(so)])
                    S.add("dve", lambda e, so=so, slot=slot, r=r, hf=hf: e.tensor_tensor(
                        out=x_all[0:r, slot, hf * 512:(hf + 1) * 512], in0=pfull(so, r, 512),
                        in1=x_all[0:r, slot, hf * 512:(hf + 1) * 512], op=ALU.add),
                        reads=[pk(so), ("x", slot)], writes=[("x", slot)])

        def final_stage(blk):
            nt = len(blk.tiles)
            rows = blk.tiles[0][1]
            for j, (slot, r) in enumerate(blk.tiles):
                S.add("act", lambda e, j=j, slot=slot, r=r: e.activation(
                    out=h_tm[0:r, j, :], in_=x_all[0:r, slot, :], func=AF.Square, accum_out=ss[0:r, j:j + 1]),
                    reads=[("x", slot)], writes=[("htm", j), "ss"], multi=True)
            rstd_from_sumsq(rows, nt)
            for j, (slot, r) in enumerate(blk.tiles):
                S.add("dve", lambda e, j=j, slot=slot, r=r: e.scalar_tensor_tensor(
                    out=x_all[0:r, slot, :], in0=x_all[0:r, slot, :], scalar=rstd[0:r, j:j + 1], in1=fg_b[0:r, :],
                    op0=ALU.mult, op1=ALU.mult),
                    reads=[("x", slot), "rstd", "fg_b"], writes=[("x", slot)])
                if blk.sample:
                    S.add("sp", lambda e, slot=slot, r=r: e.dma_start(out=y_s[:, :], in_=x_all[0:r, slot, :]),
                          reads=[("x", slot)], writes=[("y", slot)], dma=True)
                else:
                    S.add("sp", lambda e, slot=slot, r=r: e.dma_start(out=y_p[slot * 128:(slot + 1) * 128, :],
                                                                      in_=x_all[0:r, slot, :]),
                          reads=[("x", slot)], writes=[("y", slot)], dma=True)

        eps_col = sb("eps_col", [128, 1])
        S.add("pool", lambda e: e.memset(eps_col[:], EPS), writes=["eps"])

        blocks = []
        for b in range(npb):
            blocks.append(Blk(NB, [(2 * b, 128), (2 * b + 1, 128)], False, b == 0, b == npb - 1, b))
        blocks.append(Blk(NS, [(16, NS)], True, False, False, -1))

        for l in range(depth):
            layer_setup_A(l)
            if l == 0:
                load_x(4, 17, q_="pool")
            if l == 0:
                layer_setup_Bdma(l)
            if l == 0:
                H_elem(blocks[0])
                H_pe(blocks[0])
                H_evac(blocks[0], l)
            pre_conv = layer_setup_Bparams(l)
            main_early(l, blocks[0])
            pre_conv()
            pre_conv = None
            layer_setup_Bcomp(l)
            if l == 0:
                S.add("pool", lambda e: e.dma_start(out=fg_b[:], in_=bass.AP(final_g.tensor, 0, [[0, 128], [1, D]])),
                      writes=["fg_b"], dma=True)
            if l + 1 < depth:
                weight_prestage(l + 1)
            for bi, blk in enumerate(blocks):
                nxt = blocks[bi + 1] if bi + 1 < len(blocks) else None
                refill = (nxt is None and l + 1 < depth)
                main_late(l, blk, nxt, pre_conv if bi == 0 else None,
                          (lambda: weight_loads(l + 1, ("late",))) if refill else None,
                          (blocks[0], l + 1) if refill else None)
                if nxt is not None:
                    if nxt.last:
                        sample_hist_a_dma(l)
                    if nxt.sample:
                        sample_hist_z(l)
                    main_early(l, nxt)
                    if nxt.last:
                        sample_hist_a_pe(l)
                        sample_hist_z_dma(l)
                    if bi + 2 == len(blocks) and l + 1 < depth:
                        weight_loads(l + 1, ("early",))
                if refill:
                    layer_setup_Bdma(l + 1, "pool")
                O_stage(l, blk)
                if refill:
                    weight_loads(l + 1, ("wout",))
                if l == depth - 1:
                    final_stage(blk)

        with nc.Block() as block:
            S.emit(nc, stack, block)
    return nc


_CACHE = {}


def _get_program():
    if "nc" not in _CACHE:
        _CACHE["nc"] = build_program()
    return _CACHE["nc"]


def kernel(x_prompt, x_sample, state_a_conv, state_b_conv, norm_g, w_in, a_conv_w, a_conv_b, a_ln_g, a_ln_b,
           b_conv_w, c_ln_g, c_ln_b, c_ws, c_bs, w_out, final_g):
    f = lambda a: np.ascontiguousarray(np.asarray(a, dtype=np.float32))
    x_prompt, x_sample = f(x_prompt), f(x_sample)
    state_a_conv, state_b_conv = f(state_a_conv), f(state_b_conv)
    shared = {
        "norm_g": f(norm_g).reshape(DEPTH, 8, 128),
        "w_in": f(w_in),
        "a_conv_w": f(a_conv_w),
        "a_conv_b": f(a_conv_b).reshape(DEPTH, 1, WA),
        "a_ln_g": f(a_ln_g).reshape(DEPTH, 1, WA),
        "a_ln_b": f(a_ln_b).reshape(DEPTH, 1, WA),
        "b_conv_w": f(b_conv_w),
        "c_ln_g": f(c_ln_g).reshape(DEPTH, 1, WC),
        "c_ln_b": f(c_ln_b).reshape(DEPTH, 1, WC),
        "c_ws": f(c_ws),
        "c_bs": f(c_bs),
        "w_out": f(w_out),
        "final_g": f(final_g).reshape(1, D),
    }
    in_maps = []
    for i in range(NCORES):
        m = dict(shared)
        m["x_p"] = x_prompt[i]
        m["x_s"] = np.ascontiguousarray(x_sample[NSEG * i:NSEG * (i + 1)].reshape(NS, D))
        m["st_a"] = np.ascontiguousarray(state_a_conv[:, NSEG * i:NSEG * (i + 1)])
        m["st_b"] = np.ascontiguousarray(state_b_conv[:, NSEG * i:NSEG * (i + 1)])
        in_maps.append(m)
    nc = _get_program()
    res = run_bass_kernel_spmd(nc, in_maps, core_ids=list(range(NCORES)))
    R = res.results
    y_prompt = np.stack([R[i]["y_p"] for i in range(NCORES)], axis=0)
    y_sample = np.concatenate([R[i]["y_s"].reshape(NSEG, DSEQ, D) for i in range(NCORES)], axis=0)
    sa_p = np.stack([R[i]["sa_p"] for i in range(NCORES)], axis=1)
    sb_p = np.stack([R[i]["sb_p"] for i in range(NCORES)], axis=1)
    sa_s = np.concatenate([R[i]["sa_s"] for i in range(NCORES)], axis=1)
    sb_s = np.concatenate([R[i]["sb_s"] for i in range(NCORES)], axis=1)
    sv_s = np.concatenate([R[i]["sv_s"] for i in range(NCORES)], axis=1)
    out = (y_prompt, y_sample, sa_p, sb_p, sa_s, sb_s, sv_s)
    return tuple(np.ascontiguousarray(o, dtype=np.float32) for o in out)
```

```python
import numpy as np
from contextlib import ExitStack

import concourse.bass as bass
import concourse.mybir as mybir
from concourse.bass_utils import run_bass_kernel_spmd

F32 = mybir.dt.float32
BF16 = mybir.dt.bfloat16
AF = mybir.ActivationFunctionType
ALU = mybir.AluOpType

NCORES = 8
D = 1024
INW = 3456
WA = 384
WC = 256
DEPTH = 2
SEQ = 2048
NSEG = 16
DSEQ = 4
NS = NSEG * DSEQ
KA = 31
KB = 3
NB = 256
NPB = SEQ // NB
EPS = 1e-6
C_AV, C_AG, C_AZ = 0, 384, 768
C_BH, C_BB, C_BC, C_BZ = 1152, 1536, 1920, 2304
C_CU, C_CV, C_CZ = 2688, 2944, 3200
PIECES = [(C_AV, 384), (C_AG, 384), (C_BH, 384), (C_BC, 384), (C_CV, 256), (C_BB, 384), (C_BZ, 384),
          (C_CU, 256), (C_CZ, 256), (C_AZ, 384)]


def piece_of(col):
    for i, (c0, w) in enumerate(PIECES):
        if c0 <= col < c0 + w:
            return i
    raise ValueError(col)


class Op:
    __slots__ = ("eng", "fn", "deps", "multi", "dma", "idx", "sig", "prewait")

    def __init__(self, eng, fn, multi, dma, idx):
        self.eng, self.fn, self.multi, self.dma, self.idx = eng, fn, multi, dma, idx
        self.deps = set()
        self.sig = None
        self.prewait = None


class Sched:
    COMPUTE = ("pe", "act", "dve", "pool")
    MAXV = 4000
    NDMA = {"sp": 20, "pool": 8, "act": 4}

    def __init__(self):
        self.ops = []
        self.lastw = {}
        self.rd = {}

    def add(self, eng, fn, reads=(), writes=(), multi=False, dma=False):
        op = Op(eng, fn, multi, dma, len(self.ops))
        psr = [k for k in reads if isinstance(k, tuple) and k[0] == "psb"]
        if psr:
            reads = [k for k in reads if not (isinstance(k, tuple) and k[0] == "psb")]
            writes = list(writes) + psr
        for k in reads:
            w = self.lastw.get(k)
            if w is not None:
                op.deps.add(w)
        for k in writes:
            w = self.lastw.get(k)
            if w is not None:
                op.deps.add(w)
            op.deps.update(self.rd.get(k, ()))
        for k in reads:
            self.rd.setdefault(k, []).append(op.idx)
        for k in writes:
            self.lastw[k] = op.idx
            self.rd[k] = []
        op.deps.discard(op.idx)
        self.ops.append(op)
        return op

    def emit(self, nc, stack, block):
        ops = self.ops
        for op in ops:
            latest = {}
            keep = set()
            for p in op.deps:
                po = ops[p]
                if po.dma:
                    keep.add(p)
                    continue
                if po.eng == "pe" and op.eng == "pe" and not op.dma:
                    continue
                if po.eng not in latest or latest[po.eng] < p:
                    latest[po.eng] = p
            keep.update(latest.values())
            op.deps = keep
        needed = set()
        for op in ops:
            needed.update(op.deps)
        queues = {}
        for op in ops:
            queues.setdefault(op.eng, []).append(op)
        sems = {}

        def getsem(name):
            if name not in sems:
                sems[name] = stack.enter_context(nc.semaphore(name))
            return sems[name]

        for eng in self.COMPUTE:
            cnt = 0
            for op in queues.get(eng, []):
                if op.dma:
                    continue
                if op.idx in needed:
                    e, v = divmod(cnt, self.MAXV)
                    getsem("c_%s_%d" % (eng, e))
                    op.sig = ("c_%s_%d" % (eng, e), v + 1, 1)
                    cnt += 1
        all_dma_final = {}
        for eng, q in queues.items():
            n = self.NDMA.get(eng, 4)
            uses = [0] * n
            rr = 0
            for op in q:
                if not op.dma:
                    continue
                s = "d_%s_%d" % (eng, rr)
                getsem(s)
                if uses[rr] > 0:
                    op.prewait = (s, 16 * uses[rr])
                uses[rr] += 1
                op.sig = (s, 16 * uses[rr], 16)
                all_dma_final[s] = 16 * uses[rr]
                rr = (rr + 1) % n
        handles = {"pe": block.tensor, "act": block.scalar, "dve": block.vector, "pool": block.gpsimd,
                   "sp": block.sync}
        order = ["sp", "pool", "act", "dve", "pe"]
        for eng in order:
            q = queues.get(eng, [])

            def body(e, q=q, eng=eng):
                seen = {}
                for op in q:
                    waits = {}
                    for p in op.deps:
                        s, v, _ = ops[p].sig
                        if waits.get(s, 0) < v:
                            waits[s] = v
                    if op.prewait is not None:
                        s, v = op.prewait
                        if waits.get(s, 0) < v:
                            waits[s] = v
                    wl = [(s, v) for s, v in waits.items() if seen.get(s, 0) < v]
                    for s, v in wl:
                        seen[s] = v
                    attach = None
                    if wl and not op.multi:
                        attach = wl.pop()
                    for s, v in wl:
                        e.wait_ge(sems[s], v)
                    ins = op.fn(e)
                    if attach is not None:
                        ins._wait_ge(sems[attach[0]], attach[1])
                    if op.sig is not None:
                        ins.then_inc(sems[op.sig[0]], op.sig[2])
                if eng == "sp":
                    for s, v in all_dma_final.items():
                        if seen.get(s, 0) < v:
                            e.wait_ge(sems[s], v)

            handles[eng](body)


class Blk:
    def __init__(self, ntok, tiles, sample, first, last, pidx):
        self.ntok, self.tiles, self.sample, self.first, self.last, self.pidx = ntok, tiles, sample, first, last, pidx


def build_program(depth=DEPTH, npb=NPB):
    nc = bass.Bass("TRN2", target_bir_lowering=False)
    S = Sched()

    def din(name, shape):
        return nc.dram_tensor(name, shape, F32, kind="ExternalInput").ap()

    def dout(name, shape):
        return nc.dram_tensor(name, shape, F32, kind="ExternalOutput").ap()

    x_p = din("x_p", [npb * NB, D])
    x_s = din("x_s", [NS, D])
    st_a = din("st_a", [DEPTH, NSEG, KA - 1, WA])
    st_b = din("st_b", [DEPTH, NSEG, KB - 1, WA])
    norm_g = din("norm_g", [DEPTH, 8, 128])
    w_in = din("w_in", [DEPTH, D, INW])
    a_conv_w = din("a_conv_w", [DEPTH, KA, WA])
    a_conv_b = din("a_conv_b", [DEPTH, 1, WA])
    a_ln_g = din("a_ln_g", [DEPTH, 1, WA])
    a_ln_b = din("a_ln_b", [DEPTH, 1, WA])
    b_conv_w = din("b_conv_w", [DEPTH, KB, WA])
    c_ln_g = din("c_ln_g", [DEPTH, 1, WC])
    c_ln_b = din("c_ln_b", [DEPTH, 1, WC])
    c_ws = din("c_ws", [DEPTH, 4, 128, 128])
    c_bs = din("c_bs", [DEPTH, 4, 128])
    w_out = din("w_out", [DEPTH, D, D])
    final_g = din("final_g", [1, D])

    y_p = dout("y_p", [npb * NB, D])
    y_s = dout("y_s", [NS, D])
    sa_p = dout("sa_p", [DEPTH, KA - 1, WA])
    sb_p = dout("sb_p", [DEPTH, KB - 1, WA])
    sa_s = dout("sa_s", [DEPTH, NSEG, KA - 1, WA])
    sb_s = dout("sb_s", [DEPTH, NSEG, KB - 1, WA])
    sv_s = dout("sv_s", [DEPTH, NSEG, DSEQ, WC])

    w_in_scr = nc.dram_tensor("w_in_scr", [128, 8, INW], BF16).ap()
    w_out_scr = nc.dram_tensor("w_out_scr", [128, 8, D], BF16).ap()
    stack = ExitStack()
    with stack:
        def sb(name, shape, dt=F32):
            return stack.enter_context(nc.sbuf_tensor(name, shape, dt))

        def ps(name, shape, dt=F32):
            return stack.enter_context(nc.psum_tensor(name, shape, dt))

        x_all = sb("x_all", [128, 17, D])
        w_in_sb = sb("w_in_sb", [128, 8, INW], BF16)
        w_out_sb = sb("w_out_sb", [128, 8, D], BF16)
        W4 = sb("W4", [128, 12, 8, 32], BF16)
        A4 = sb("A4", [128, 12, NB + 28], BF16)
        id32 = sb("id32", [128, 32], BF16)
        wcol4 = sb("wcol4", [128, 12, 8])
        colsK = sb("colsK", [128, 3, 4, 8])
        h_tm = sb("h_tm", [128, 2, D], BF16)
        h_fm = sb("h_fm", [128, 8, NB], BF16)
        mix_fm = sb("mix_fm", [128, 8, NB], BF16)
        a_ext = sb("a_ext", [128, 3, NSEG * 34 + 4], BF16)
        z_ext = sb("z_ext", [128, 3, NB + 4], BF16)
        th = sb("th", [128, 2, NB])
        cb = sb("cb", [128, 3, NB])
        cbh = sb("cbh", [128, 3, NB], BF16)
        sq = sb("sq", [128, 3, NB], BF16)
        mean_b = sb("mean_b", [128, NB])
        vb = sb("vb", [128, NB])
        szt = sb("szt", [128, 3, NB])
        bh = sb("bh", [128, 2, NB])
        bb = sb("bb", [128, 2, NB])
        szb = sb("szb", [128, 2, NB])
        ncv = sb("ncv", [128, 2, WC])
        vn_bf = sb("vn_bf", [128, 2, WC], BF16)
        tC = sb("tC", [128, 2, NB])
        szc = sb("szc", [128, 2, NB])
        a32 = sb("a32", [128, 3, NS])
        z32 = sb("z32", [128, 3, 32])
        ident_bf = sb("ident_bf", [128, 128], BF16)
        ident_f = sb("ident_f", [128, 128])
        maskT = sb("maskT", [128, 128])
        maskB = sb("maskB", [64, NSEG, 4])
        ones_bf = sb("ones_bf", [128, 128], BF16)
        wmT = sb("wmT", [128, 4, 128], BF16)
        wblk = sb("wblk", [64, 4, 64], BF16)
        cbias = sb("cbias", [128, 2, 128])
        gC_b = sb("gC_b", [128, WC])
        bC_b = sb("bC_b", [128, WC])
        fg_b = sb("fg_b", [128, D])
        colsA = sb("colsA", [128, 3, 40])
        colsH = sb("colsH", [128, 3, 32])
        g_col = sb("g_col", [128, DEPTH * 8])
        stG_all = sb("stG_all", [DEPTH * 8, 128])
        ss = sb("ss", [128, 2])
        rstd = sb("rstd", [128, 2])
        st6 = sb("st6", [128, 2, 6])
        mv = sb("mv", [128, 2, 2])
        rstdc = sb("rstdc", [128, 2])

        NBANK = 8
        pbank = [ps("pb%d" % i, [128, 512]) for i in range(NBANK)]
        hT_state = {}

        state = {"bank": 0}

        def alloc_bank():
            b = state["bank"]
            state["bank"] = (b + 1) % NBANK
            return b

        def pap(bh_, rows=128, n=NB, p0=0):
            b, hf = bh_
            return pbank[b][p0:p0 + rows, hf * 256:hf * 256 + n]

        def pfull(b, rows=128, n=512):
            return pbank[b][0:rows, 0:n]

        def pk(x):
            return ("psb", x[0] if isinstance(x, tuple) else x)

        ones_f = cb[:, 0, 0:128]
        S.add("pool", lambda e: e.memset(ones_f, 1.0), writes=[("cb", 0)])
        S.add("pool", lambda e: e.memset(ones_bf[:], 1.0), writes=["ones_bf"])
        S.add("pool", lambda e: e.affine_select(out=ident_f[:], in_=ones_f, pattern=[[-1, 128]],
                                                compare_op=ALU.is_equal, fill=0.0, base=0, channel_multiplier=1),
              reads=[("cb", 0)], writes=["ident_f"])
        S.add("pool", lambda e: e.affine_select(out=maskT[:], in_=ones_f, pattern=[[-1, 128]],
                                                compare_op=ALU.is_ge, fill=0.0, base=0, channel_multiplier=1),
              reads=[("cb", 0)], writes=["maskT"])
        S.add("pool", lambda e: e.affine_select(out=maskB[:], in_=ones_f[0:64, 0:64].rearrange("p (g s) -> p g s", s=4),
                                                pattern=[[-4, NSEG], [-1, 4]],
                                                compare_op=ALU.is_ge, fill=0.0, base=0, channel_multiplier=1),
              reads=[("cb", 0)], writes=["maskB"])
        S.add("pool", lambda e: e.affine_select(out=maskB[:], in_=maskB[:], pattern=[[4, NSEG], [0, 4]],
                                                compare_op=ALU.is_ge, fill=0.0, base=3, channel_multiplier=-1),
              reads=["maskB"], writes=["maskB"])
        S.add("pool", lambda e: e.tensor_copy(ident_bf[:], ident_f[:]), reads=["ident_f"], writes=["ident_bf"])
        S.add("pool", lambda e: e.tensor_tensor(out=id32[:], in0=ident_bf[:, 0:32], in1=ident_bf[:, 32:64], op=ALU.add),
              reads=["ident_bf"], writes=["id32"])
        S.add("pool", lambda e: e.tensor_tensor(out=id32[:], in0=id32[:], in1=ident_bf[:, 64:96], op=ALU.add),
              reads=["ident_bf", "id32"], writes=["id32"])
        S.add("pool", lambda e: e.tensor_tensor(out=id32[:], in0=id32[:], in1=ident_bf[:, 96:128], op=ALU.add),
              reads=["ident_bf", "id32"], writes=["id32"])
        S.add("pool", lambda e: e.memset(colsH[:], 0.0), writes=["colsH"])
        S.add("sp", lambda e: e.dma_start(out=stG_all[:], in_=norm_g.rearrange("l k p -> (l k) p")),
              writes=["stG_all"], dma=True)
        S.add("pool", lambda e: e.memset(A4[:], 0.0), writes=[("a4", b) for b in range(12)])
        S.add("pool", lambda e: e.memset(a_ext[:], 0.0), writes=[("aext", 0), ("aext", 1), ("aext", 2)])
        S.add("sp", lambda e: e.dma_start(out=fg_b[:], in_=bass.AP(final_g.tensor, 0, [[0, 128], [1, D]])),
              writes=["fg_b"], dma=True)
        def load_x(j0, j1, q_="sp"):
            for j in range(j0, j1):
                if j < 2 * npb:
                    S.add(q_, lambda e, j=j: e.dma_start(out=x_all[:, j, :], in_=x_p[j * 128:(j + 1) * 128, :]),
                          writes=[("x", j)], dma=True)
            if j1 > 2 * npb:
                S.add(q_, lambda e: e.dma_start(out=x_all[0:NS, 16, :], in_=x_s[:, :]),
                      writes=[("x", 16)], dma=True)

        load_x(0, 4)

        cp_state = {"i": 0}

        def copy_op(out, in_, reads, writes, scale=None, eng=None):
            if eng is None:
                eng = "act" if cp_state["i"] % 2 == 0 else "dve"
                cp_state["i"] += 1
            if eng == "act":
                if scale is None:
                    S.add("act", lambda e: e.activation(out=out, in_=in_, func=AF.Copy), reads=reads, writes=writes)
                else:
                    S.add("act", lambda e: e.activation(out=out, in_=in_, func=AF.Copy, scale=scale),
                          reads=reads, writes=writes)
            else:
                if scale is None:
                    S.add("dve", lambda e: e.tensor_copy(out, in_), reads=reads, writes=writes)
                else:
                    S.add("dve", lambda e: e.tensor_scalar(out, in_, scale, None, ALU.mult),
                          reads=reads, writes=writes)

        def pe_transpose(out, in_, ident, reads, writes):
            S.add("pe", lambda e: e.transpose(out, in_, ident), reads=reads, writes=writes)

        _sg = (alloc_bank(), 0)
        pe_transpose(pap(_sg, 128, DEPTH * 8), stG_all[:], ident_f[0:DEPTH * 8, 0:DEPTH * 8], ["stG_all", "ident_f"],
                     [pk(_sg)])
        copy_op(g_col[:, :], pap(_sg, 128, DEPTH * 8), [pk(_sg)], ["g_col"], eng="act")

        K_TH = [("thb", 0), ("thb", 1)]
        K_CB = [("cb", 0), ("cb", 1), ("cb", 2)]
        K_SZT = [("sztb", 0), ("sztb", 1), ("sztb", 2)]
        K_BH = [("bhb", 0), ("bhb", 1)]
        K_BB = [("bbb", 0), ("bbb", 1)]
        K_SZB = [("szbb", 0), ("szbb", 1)]

        def weight_loads(l, which=("early", "late", "wout")):
            wkeys = []
            if l == 0:
                w_in_v = w_in[l].rearrange("(k p) e -> p k e", p=128)
                w_out_v = w_out[l].rearrange("(k p) e -> p k e", p=128)
            else:
                w_in_v, w_out_v = w_in_scr, w_out_scr
            for pi, (c0, w) in enumerate(PIECES):
                if ("early" if pi < 5 else "late") not in which:
                    continue
                S.add("pool", lambda e, c0=c0, w=w: e.dma_start(out=w_in_sb[:, :, c0:c0 + w], in_=w_in_v[:, :, c0:c0 + w]),
                      reads=(wkeys[-3:-2] if l == 0 else [("wscr", pi)]), writes=[("win", pi)], dma=True)
                wkeys.append(("win", pi))
            if "wout" in which:
                for hf in range(2):
                    S.add("pool", lambda e, hf=hf: e.dma_start(out=w_out_sb[:, :, hf * 512:(hf + 1) * 512],
                                                               in_=w_out_v[:, :, hf * 512:(hf + 1) * 512]),
                          reads=(wkeys[-3:-2] if l == 0 else [("wscr_o", hf)]), writes=[("wout", hf)], dma=True)
                    wkeys.append(("wout", hf))

        def weight_prestage(l):
            w_in_v = w_in[l].rearrange("(k p) e -> p k e", p=128)
            w_out_v = w_out[l].rearrange("(k p) e -> p k e", p=128)
            for pi, (c0, w) in enumerate(PIECES):
                S.add("pool", lambda e, c0=c0, w=w: e.dma_start(out=w_in_scr[:, :, c0:c0 + w],
                                                                in_=w_in_v[:, :, c0:c0 + w]),
                      writes=[("wscr", pi)], dma=True)
            for hf in range(2):
                S.add("pool", lambda e, hf=hf: e.dma_start(out=w_out_scr[:, :, hf * 512:(hf + 1) * 512],
                                                           in_=w_out_v[:, :, hf * 512:(hf + 1) * 512]),
                      writes=[("wscr_o", hf)], dma=True)

        def layer_setup_A(l):
            if l == 0:
                weight_loads(l)
            S.add("sp", lambda e: e.dma_start(out=gC_b[:], in_=bass.AP(c_ln_g.tensor, l * WC, [[0, 128], [1, WC]])),
                  writes=["gC_b"], dma=True)
            S.add("sp", lambda e: e.dma_start(out=bC_b[:], in_=bass.AP(c_ln_b.tensor, l * WC, [[0, 128], [1, WC]])),
                  writes=["bC_b"], dma=True)
            S.add("dve", lambda e: e.memset(a_ext[:, :, 0:KA - 1], 0.0), writes=[("aext", 0), ("aext", 1), ("aext", 2)])
            S.add("dve", lambda e: e.memset(z_ext[:, :, 0:KB - 1], 0.0), writes=[("zext", 0), ("zext", 1), ("zext", 2)])

        def layer_setup_Bdma(l, q_="sp"):
            thf = cb[:].rearrange("p a b -> p (a b)")
            sztf = szt[:].rearrange("p a b -> p (a b)")
            stP = thf[0:37, 0:WA]
            srcs = [(a_conv_w[l], 0, KA), (a_conv_b[l], KA, 1), (a_ln_g[l], KA + 1, 1), (a_ln_b[l], KA + 2, 1),
                    (b_conv_w[l], KA + 3, KB)]
            for i_, (src, r0, nr) in enumerate(srcs):
                S.add(q_, lambda e, src=src, r0=r0, nr=nr: e.dma_start(out=thf[r0:r0 + nr, 0:WA], in_=src),
                      writes=(K_CB if i_ == 0 else []) + [("stP", i_)], dma=True)
            wsn = sztf[:, 0:512].rearrange("p (h s) -> p h s", h=4)
            S.add(q_, lambda e: e.dma_start(out=wsn, in_=c_ws[l].rearrange("h t s -> t h s")),
                  writes=K_SZT, dma=True)
            wrc = vb[0:64, 128:144].rearrange("p (h s) -> p h s", h=4)
            for g in range(NSEG):
                src = bass.AP(c_ws.tensor, l * 4 * 128 * 128, [[128, 4], [128 * 128, 4], [1, 4]])
                S.add(q_, lambda e, g=g, src=src: e.dma_start(out=wrc[4 * g:4 * g + 4], in_=src),
                      writes=(["vb"] if g == 0 else []) + [("wrc", g)], dma=True)
            for h in range(4):
                src = bass.AP(c_bs.tensor, (l * 4 + h) * 128, [[0, 64], [1, 128]])
                S.add(q_, lambda e, h=h, src=src: e.dma_start(out=cbias[(h % 2) * 64:(h % 2) * 64 + 64, h // 2, :],
                                                              in_=src), writes=[("cbias", h)], dma=True)
            S.add(q_, lambda e: e.dma_start(out=sa_s[l, :, 0:KA - 1 - DSEQ, :], in_=st_a[l, :, DSEQ:KA - 1, :]),
                  writes=[("sa_s_copy", l)], dma=True)

        def layer_setup_Bparams(l):
            thf = cb[:].rearrange("p a b -> p (a b)")
            stP = thf[0:37, 0:WA]
            for c in range(3):
                s = (alloc_bank(), 0)
                pe_transpose(pap(s, 128, 37), stP[:, c * 128:(c + 1) * 128], ident_f[0:37, 0:37],
                             K_CB + [("stP", i_) for i_ in range(5)] + ["ident_f"], [pk(s)])
                copy_op(colsA[:, c, 0:37], pap(s, 128, 37), [pk(s)], ["colsA"], eng="act")
            S.add("dve", lambda e: e.tensor_scalar(colsH[:, :, 0:KA], colsA[:, :, 0:KA], 0.5, None, ALU.mult),
                  reads=["colsA"], writes=["colsH"])
            S.add("dve", lambda e: e.tensor_copy(colsK[:], colsH[:, :, 0:32].rearrange("p c (m j) -> p c j m", j=4)),
                  reads=["colsH"], writes=["colsK"])
            for q in range(4):
                for j in range(4):
                    S.add("sp", lambda e, q=q, j=j: e.dma_start(
                        out=wcol4[32 * j:32 * j + 32, :, :].rearrange("p (c q) m -> p c q m", q=4)[:, :, q, :],
                        in_=colsK[32 * q:32 * q + 32, :, j, :]),
                        reads=["colsK"], writes=[("wcol4", j, q)], dma=True)
            sztf_ = szt[:].rearrange("p a b -> p (a b)")
            wsn = sztf_[:, 0:512].rearrange("p (h s) -> p h s", h=4)
            wrc = vb[0:64, 128:144].rearrange("p (h s) -> p h s", h=4)
            wr = mean_b[0:64, :].rearrange("p (h g s) -> p h g s", h=4, g=NSEG)
            S.add("dve", lambda e: e.tensor_tensor(out=wsn, in0=wsn, in1=maskT[:].unsqueeze(1).broadcast_to([128, 4, 128]),
                                                   op=ALU.mult),
                  reads=K_SZT + ["maskT"], writes=K_SZT)
            S.add("dve", lambda e: e.tensor_tensor(out=wr, in0=wrc.unsqueeze(2).broadcast_to([64, 4, NSEG, 4]),
                                                   in1=maskB[:].unsqueeze(1).broadcast_to([64, 4, NSEG, 4]), op=ALU.mult),
                  reads=["vb", "maskB"] + [("wrc", g) for g in range(NSEG)], writes=["mean_b"])
            def w4_build():
                WC4 = [("wcol4", j, q) for j in range(4) for q in range(4)]
                S.add("dve", lambda e: e.tensor_tensor(
                    out=W4[:].rearrange("p b m c -> p (b m) c"),
                    in0=id32[:].unsqueeze(1).broadcast_to([128, 96, 32]),
                    in1=wcol4[:].rearrange("p b m -> p (b m)").unsqueeze(2).broadcast_to([128, 96, 32]), op=ALU.mult),
                    reads=WC4 + ["id32"], writes=["w4"])
            return w4_build

        def layer_setup_Bcomp(l):
            thf = cb[:].rearrange("p a b -> p (a b)")
            sztf = szt[:].rearrange("p a b -> p (a b)")
            stP = thf[0:37, 0:WA]
            wsn = sztf[:, 0:512].rearrange("p (h s) -> p h s", h=4)
            wrc = vb[0:64, 128:144].rearrange("p (h s) -> p h s", h=4)
            for h in range(4):
                s = (alloc_bank(), 0)
                pe_transpose(pap(s, 128, 128), wsn[:, h, :], ident_f[:], K_SZT + ["ident_f"], [pk(s)])
                copy_op(wmT[:, h, :], pap(s, 128, 128), [pk(s)], ["wmT"])
            wr = mean_b[0:64, :].rearrange("p (h g s) -> p h g s", h=4, g=NSEG)
            wr3 = mean_b[0:64, :].rearrange("p (h q) -> p h q", h=4)
            for h in range(4):
                s = (alloc_bank(), 0)
                pe_transpose(pap(s, 64, 64), wr3[:, h, :], ident_f[0:64, 0:64], ["mean_b", "ident_f"], [pk(s)])
                copy_op(wblk[:, h, :], pap(s, 64, 64), [pk(s)], ["wblk"])

        def _hist_stages():
            return [(szb[:].rearrange("p a b -> p (a b)"), K_SZB),
                    (tC[:].rearrange("p a b -> p (a b)"), [("tC", 0), ("tC", 1)]),
                    (szc[:].rearrange("p a b -> p (a b)"), [("szc", 0), ("szc", 1)]),
                    (bb[:].rearrange("p a b -> p (a b)"), K_BB)]

        def sample_hist_a_dma(l):
            for q, (stg, key) in enumerate(_hist_stages()):
                src = st_a[l, 4 * q:4 * q + 4].rearrange("g r c -> (g r) c")
                S.add("sp", lambda e, stg=stg, src=src: e.dma_start(out=stg[0:120, 0:WA], in_=src),
                      writes=key, dma=True)

        def sample_hist_a_pe(l):
            aS = a_ext[:, :, 0:NSEG * 34].rearrange("p c (g r) -> p c g r", r=34)
            for q, (stg, key) in enumerate(_hist_stages()):
                for c in range(3):
                    s = (alloc_bank(), 0)
                    pe_transpose(pap(s, 128, 120), stg[0:120, c * 128:(c + 1) * 128], ident_f[0:120, 0:120],
                                 key + ["ident_f"], [pk(s)])
                    copy_op(aS[:, c, 4 * q:4 * q + 4, 0:KA - 1],
                            pap(s, 128, 120).rearrange("p (g r) -> p g r", r=KA - 1), [pk(s)], [("aext", c)],
                            scale=2.0)

        K_NCV = [("ncv", 0), ("ncv", 1)]

        def sample_hist_z_dma(l):
            ncvf = ncv[:].rearrange("p a b -> p (a b)")
            S.add("sp", lambda e: e.dma_start(out=ncvf[0:32, 0:WA], in_=st_b[l].rearrange("g r c -> (g r) c")),
                  writes=K_NCV, dma=True)

        def sample_hist_z(l):
            ncvf = ncv[:].rearrange("p a b -> p (a b)")
            zS = z_ext[:, :, 0:NSEG * 6].rearrange("p c (g r) -> p c g r", r=6)
            for c in range(3):
                s = (alloc_bank(), 0)
                pe_transpose(pap(s, 128, 32), ncvf[0:32, c * 128:(c + 1) * 128], ident_f[0:32, 0:32],
                             K_NCV + ["ident_f"], [pk(s)])
                copy_op(zS[:, c, :, 0:KB - 1], pap(s, 128, 32).rearrange("p (g r) -> p g r", r=KB - 1),
                        [pk(s)], [("zext", c)])

        def rstd_from_sumsq(rows, nt):
            S.add("act", lambda e: e.activation(out=rstd[0:rows, 0:nt], in_=ss[0:rows, 0:nt], func=AF.Sqrt,
                                                bias=eps_col[0:rows, :], scale=1.0 / D),
                  reads=["ss", "eps"], writes=["rstd"])
            S.add("dve", lambda e: e.reciprocal(rstd[0:rows, 0:nt], rstd[0:rows, 0:nt]),
                  reads=["rstd"], writes=["rstd"])

        def H_elem(blk):
            nt = len(blk.tiles)
            rows = blk.tiles[0][1]
            for j, (slot, r) in enumerate(blk.tiles):
                S.add("act", lambda e, j=j, slot=slot, r=r: e.activation(
                    out=h_tm[0:r, j, :], in_=x_all[0:r, slot, :], func=AF.Square, accum_out=ss[0:r, j:j + 1]),
                    reads=[("x", slot)], writes=[("htm", j), "ss"], multi=True)
            rstd_from_sumsq(rows, nt)
            for j, (slot, r) in enumerate(blk.tiles):
                S.add("dve", lambda e, j=j, slot=slot, r=r: e.tensor_scalar(
                    h_tm[0:r, j, :], x_all[0:r, slot, :], rstd[0:r, j:j + 1], None, ALU.mult),
                    reads=[("x", slot), "rstd"], writes=[("htm", j)])

        def hT_view(bank):
            return pbank[bank][:].bitcast(BF16).rearrange("p (k t) -> p k t", t=NB)

        def H_pe(blk):
            banks = (alloc_bank(), alloc_bank())
            hT_state["banks"] = banks
            for j, (slot, r) in enumerate(blk.tiles):
                for k in range(8):
                    pe_transpose(hT_view(banks[k // 4])[:, k % 4, j * 128:j * 128 + r], h_tm[0:r, j, k * 128:(k + 1) * 128],
                                 ident_bf[0:r, 0:r], [("htm", j), "ident_bf"], [pk(banks[k // 4])])

        def H_evac(blk, l):
            n = blk.ntok
            banks = hT_state["banks"]
            for k in range(8):
                if k < 4:
                    S.add("act", lambda e, k=k: e.activation(out=h_fm[:, k, 0:n], in_=hT_view(banks[0])[:, k % 4, 0:n],
                                                            func=AF.Copy, scale=g_col[:, l * 8 + k:l * 8 + k + 1]),
                          reads=[pk(banks[0]), "g_col"], writes=[("hfm", k)])
                else:
                    S.add("dve", lambda e, k=k: e.tensor_scalar(h_fm[:, k, 0:n], hT_view(banks[1])[:, k % 4, 0:n],
                                                               g_col[:, l * 8 + k:l * 8 + k + 1], None, ALU.mult),
                          reads=[pk(banks[1]), "g_col"], writes=[("hfm", k)])

        HFM = [("hfm", k) for k in range(8)]

        def proj_fm(col0, n, s):
            pi = piece_of(col0)
            for k in range(8):
                S.add("pe", lambda e, k=k, s=s: e.matmul(pap(s, 128, n), w_in_sb[:, k, col0:col0 + 128],
                                                        h_fm[:, k, 0:n], start=(k == 0), stop=(k == 7)),
                      reads=[("win", pi)] + HFM, writes=[pk(s)])
            return s

        def main_early(l, blk):
            n = blk.ntok
            smp = blk.sample
            if smp:
                aS = a_ext[:, :, 0:NSEG * 34].rearrange("p c (g r) -> p c g r", r=34)
                zS = z_ext[:, :, 0:NSEG * 6].rearrange("p c (g r) -> p c g r", r=6)

                def v3(ap2):
                    return ap2.rearrange("p (g i) -> p g i", i=DSEQ)

                def awin(c, k):
                    return aS[:, c, :, k:k + DSEQ]

                def zwin(c, k):
                    return zS[:, c, :, k:k + DSEQ]
            else:
                def v3(ap2):
                    return ap2

                def awin(c, k):
                    return a_ext[:, c, k:k + n]

                def zwin(c, k):
                    return z_ext[:, c, k:k + n]
            nt = len(blk.tiles)
            rows = blk.tiles[0][1]

            for c in range(3):
                bk = alloc_bank()
                sv = proj_fm(C_AV + c * 128, n, (bk, 0))
                sg = proj_fm(C_AG + c * 128, n, (bk, 1))
                tb = c % 2
                S.add("act", lambda e, sg=sg, tb=tb: e.activation(out=th[:, tb, 0:n], in_=pap(sg, 128, n),
                                                                  func=AF.Tanh, scale=0.5),
                      reads=[pk(sg)], writes=[("thb", tb)])
                S.add("dve", lambda e, sv=sv, tb=tb, c=c: e.scalar_tensor_tensor(
                    out=awin(c, KA - 1), in0=v3(th[:, tb, 0:n]), scalar=1.0, in1=v3(pap(sv, 128, n)),
                    op0=ALU.add, op1=ALU.mult),
                    reads=[("thb", tb), pk(sv)], writes=[("aext", c)])
                if blk.last or smp:
                    c0, nc_ = (0, NS) if smp else (n - (KA - 1), KA - 1)
                    S.add("dve", lambda e, sv=sv, tb=tb, c=c, c0=c0, nc_=nc_: e.scalar_tensor_tensor(
                        out=a32[:, c, 0:nc_], in0=th[:, tb, c0:c0 + nc_], scalar=1.0,
                        in1=pap(sv, 128, n)[:, c0:c0 + nc_], op0=ALU.add, op1=ALU.mult),
                        reads=[("thb", tb), pk(sv)], writes=[("a32", c)])
            for c in range(3):
                bk = alloc_bank()
                sh = proj_fm(C_BH + c * 128, n, (bk, 0))
                scc = proj_fm(C_BC + c * 128, n, (bk, 1))
                tb = c % 2
                S.add("act", lambda e, sh=sh, tb=tb: e.activation(out=bh[:, tb, 0:n], in_=pap(sh, 128, n), func=AF.Copy),
                      reads=[pk(sh)], writes=[("bhb", tb)])
                S.add("dve", lambda e, scc=scc, tb=tb, c=c: e.tensor_tensor(
                    out=zwin(c, KB - 1), in0=v3(pap(scc, 128, n)), in1=v3(bh[:, tb, 0:n]), op=ALU.mult),
                    reads=[("bhb", tb), pk(scc)], writes=[("zext", c)])
                if blk.last:
                    S.add("dve", lambda e, scc=scc, tb=tb, c=c: e.tensor_tensor(
                        out=z32[:, c, 0:2], in0=pap(scc, 128, n)[:, n - 2:n], in1=bh[:, tb, n - 2:n], op=ALU.mult),
                        reads=[("bhb", tb), pk(scc)], writes=[("z32", c)])
                if smp:
                    S.add("dve", lambda e, scc=scc, tb=tb, c=c: e.tensor_tensor(
                        out=z32[:, c, 0:32].rearrange("p (g i) -> p g i", i=2),
                        in0=v3(pap(scc, 128, n))[:, :, 2:4], in1=v3(bh[:, tb, 0:n])[:, :, 2:4], op=ALU.mult),
                        reads=[("bhb", tb), pk(scc)], writes=[("z32", c)])
            if smp:
                pass
            else:
                ncol = n + 28
                nmv = ncol + 4
                for c in range(3):
                    for q in range(4):
                        b = 4 * c + q
                        bk = alloc_bank()
                        for j in range(4):
                            S.add("pe", lambda e, c=c, q=q, j=j, bk=bk: e.matmul(
                                pbank[bk][32 * j:32 * j + 32, 3 - j:3 - j + nmv], ident_bf[:, 32 * q:32 * q + 32],
                                a_ext[:, c, 0:nmv], start=True, stop=True, tile_position=(0, 32 * j)),
                                reads=[("aext", c), "ident_bf"], writes=[pk(bk)])
                        copy_op(A4[:, b, 0:ncol], pbank[bk][:, 3:3 + ncol], [pk(bk)], [("a4", b)])
            scv = []
            bkv = alloc_bank()
            for j, (slot, r) in enumerate(blk.tiles):
                s = (bkv, j)
                scv.append(s)
                pi = piece_of(C_CV)
                for k in range(8):
                    S.add("pe", lambda e, k=k, s=s, j=j, r=r: e.matmul(
                        pap(s, r, WC), h_fm[:, k, j * 128:j * 128 + r], w_in_sb[:, k, C_CV:C_CV + WC],
                        start=(k == 0), stop=(k == 7)),
                        reads=[("win", pi)] + HFM, writes=[pk(s)])
            for j, (slot, r) in enumerate(blk.tiles):
                s = scv[j]
                S.add("dve", lambda e, s=s, j=j, r=r: e.bn_stats(st6[0:r, j, :], pap(s, r, WC)),
                      reads=[pk(s)], writes=["st6"])
                S.add("dve", lambda e, j=j, r=r: e.bn_aggr(mv[0:r, j, :], st6[0:r, j, :]),
                      reads=["st6"], writes=["mv"])
            S.add("act", lambda e: e.activation(out=rstdc[0:rows, 0:nt], in_=mv[0:rows, 0:nt, 1], func=AF.Sqrt,
                                                bias=eps_col[0:rows, :], scale=1.0),
                  reads=["mv", "eps"], writes=["rstdc"])
            S.add("dve", lambda e: e.reciprocal(rstdc[0:rows, 0:nt], rstdc[0:rows, 0:nt]),
                  reads=["rstdc"], writes=["rstdc"])
            for j, (slot, r) in enumerate(blk.tiles):
                s = scv[j]
                S.add("dve", lambda e, s=s, j=j, r=r: e.tensor_scalar(
                    ncv[0:r, j, :], pap(s, r, WC), mv[0:r, j, 0:1], rstdc[0:r, j:j + 1], ALU.subtract, ALU.mult),
                    reads=[pk(s), "mv", "rstdc"], writes=[("ncv", j)])
                S.add("dve", lambda e, j=j, r=r: e.tensor_tensor(out=ncv[0:r, j, :], in0=ncv[0:r, j, :],
                                                                  in1=gC_b[0:r, :], op=ALU.mult),
                      reads=[("ncv", j), "gC_b"], writes=[("ncv", j)])
                if smp:
                    S.add("dve", lambda e, j=j, r=r: e.tensor_tensor(out=ncv[0:r, j, :], in0=ncv[0:r, j, :],
                                                                      in1=bC_b[0:r, :], op=ALU.add),
                          reads=[("ncv", j), "bC_b"], writes=[("ncv", j)])
                    S.add("sp", lambda e, r=r: e.dma_start(out=sv_s[l].rearrange("g i c -> (g i) c"),
                                                           in_=ncv[0:r, 0, :]),
                          reads=[("ncv", 0)], writes=[("sv_s", l)], dma=True)
                    S.add("act", lambda e, j=j, r=r: e.activation(out=vn_bf[0:r, j, :], in_=ncv[0:r, j, :],
                                                                  func=AF.Copy),
                          reads=[("ncv", j)], writes=[("vnbf", j)])
                else:
                    S.add("dve", lambda e, j=j, r=r: e.tensor_tensor(out=vn_bf[0:r, j, :], in0=ncv[0:r, j, :],
                                                                      in1=bC_b[0:r, :], op=ALU.add),
                          reads=[("ncv", j), "bC_b"], writes=[("vnbf", j)])

        def main_late(l, blk, next_blk, pre_conv=None, after_proj=None, h_next=None):
            n = blk.ntok
            smp = blk.sample
            if smp:
                aS = a_ext[:, :, 0:NSEG * 34].rearrange("p c (g r) -> p c g r", r=34)
                zS = z_ext[:, :, 0:NSEG * 6].rearrange("p c (g r) -> p c g r", r=6)

                def v3(ap2):
                    return ap2.rearrange("p (g i) -> p g i", i=DSEQ)

                def awin(c, k):
                    return aS[:, c, :, k:k + DSEQ]

                def zwin(c, k):
                    return zS[:, c, :, k:k + DSEQ]
            else:
                def v3(ap2):
                    return ap2

                def awin(c, k):
                    return a_ext[:, c, k:k + n]

                def zwin(c, k):
                    return z_ext[:, c, k:k + n]
            nt = len(blk.tiles)
            rows = blk.tiles[0][1]
            hn_blk, hn_l = (next_blk, l) if next_blk is not None else (h_next if h_next is not None else (None, l))
            if hn_blk is not None:
                H_elem(hn_blk)
            for c in range(3):
                bk = alloc_bank()
                sbb = proj_fm(C_BB + c * 128, n, (bk, 0))
                szs = proj_fm(C_BZ + c * 128, n, (bk, 1))
                tb = c % 2
                S.add("act", lambda e, szs=szs, tb=tb: e.activation(out=szb[:, tb, 0:n], in_=pap(szs, 128, n),
                                                                    func=AF.Silu),
                      reads=[pk(szs)], writes=[("szbb", tb)])
                for k in range(KB):
                    if k == 0:
                        S.add("dve", lambda e, c=c, k=k, tb=tb: e.tensor_scalar(
                            v3(bb[:, tb, 0:n]), zwin(c, k), colsA[:, c, KA + 3 + k:KA + 4 + k], None, ALU.mult),
                            reads=[("zext", c), "colsA"], writes=[("bbb", tb)])
                    else:
                        S.add("dve", lambda e, c=c, k=k, tb=tb: e.scalar_tensor_tensor(
                            out=v3(bb[:, tb, 0:n]), in0=zwin(c, k), scalar=colsA[:, c, KA + 3 + k:KA + 4 + k],
                            in1=v3(bb[:, tb, 0:n]), op0=ALU.mult, op1=ALU.add),
                            reads=[("zext", c), "colsA", ("bbb", tb)], writes=[("bbb", tb)])
                if not smp:
                    S.add("dve", lambda e, c=c: e.tensor_copy(z_ext[:, c, 0:KB - 1], z_ext[:, c, n:n + KB - 1]),
                          reads=[("zext", c)], writes=[("zext", c)])
                S.add("dve", lambda e, sbb=sbb, tb=tb: e.tensor_tensor(out=bb[:, tb, 0:n], in0=pap(sbb, 128, n),
                                                                       in1=bb[:, tb, 0:n], op=ALU.mult),
                      reads=[pk(sbb), ("bbb", tb)], writes=[("bbb", tb)])
                S.add("dve", lambda e, c=c, tb=tb: e.tensor_tensor(out=mix_fm[:, 3 + c, 0:n], in0=bb[:, tb, 0:n],
                                                                   in1=szb[:, tb, 0:n], op=ALU.mult),
                      reads=[("bbb", tb), ("szbb", tb)], writes=[("mix", 3 + c)])
            bkz = alloc_bank()
            cslots = []
            for hc in range(2):
                bk = alloc_bank()
                scm = (bk, 0)
                for j, (slot, r) in enumerate(blk.tiles):
                    for hh in range(2):
                        h = 2 * hc + hh
                        rhs = wblk[:, h, :] if smp else wmT[:, h, :]
                        S.add("pe", lambda e, j=j, r=r, hh=hh, h=h, rhs=rhs, scm=scm: e.matmul(
                            pbank[scm[0]][64 * hh:64 * hh + 64, j * 128:j * 128 + r],
                            vn_bf[0:r, j, h * 64:(h + 1) * 64], rhs, start=True, stop=True),
                            reads=[("vnbf", j), "wblk" if smp else "wmT"], writes=[pk(scm)])
                su = proj_fm(C_CU + hc * 128, n, (bk, 1))
                cslots.append((scm, su))
            szzs = [proj_fm(C_CZ + hc * 128, n, (bkz, hc)) for hc in range(2)]
            for hc in range(2):
                scm, su = cslots[hc]
                szz = szzs[hc]
                if smp:
                    cbv = cbias[:, hc, 0:DSEQ].unsqueeze(1).broadcast_to([128, NSEG, DSEQ])
                    pv = v3(pap(scm, 128, n))
                    tv = v3(tC[:, hc, 0:n])
                else:
                    cbv = cbias[:, hc, :].unsqueeze(1).broadcast_to([128, nt, 128])
                    pv = pap(scm, 128, n).rearrange("p (j t) -> p j t", t=128)
                    tv = tC[:, hc, 0:n].rearrange("p (j t) -> p j t", t=128)
                S.add("dve", lambda e, pv=pv, tv=tv, cbv=cbv: e.tensor_tensor(out=tv, in0=pv, in1=cbv, op=ALU.add),
                      reads=[pk(scm)] + [("cbias", h_) for h_ in range(4)], writes=[("tC", hc)])
                S.add("dve", lambda e, su=su, hc=hc: e.tensor_tensor(out=tC[:, hc, 0:n], in0=pap(su, 128, n),
                                                                     in1=tC[:, hc, 0:n], op=ALU.mult),
                      reads=[pk(su), ("tC", hc)], writes=[("tC", hc)])
                S.add("act", lambda e, szz=szz, hc=hc: e.activation(out=szc[:, hc, 0:n], in_=pap(szz, 128, n),
                                                                    func=AF.Silu),
                      reads=[pk(szz)], writes=[("szc", hc)])
                S.add("dve", lambda e, hc=hc: e.tensor_tensor(out=mix_fm[:, 6 + hc, 0:n], in0=tC[:, hc, 0:n],
                                                              in1=szc[:, hc, 0:n], op=ALU.mult),
                      reads=[("tC", hc), ("szc", hc)], writes=[("mix", 6 + hc)])
            bkaz = [alloc_bank(), alloc_bank()]
            szs_ = [proj_fm(C_AZ + c * 128, n, (bkaz[c // 2], c % 2)) for c in range(3)]
            for c in range(3):
                S.add("act", lambda e, c=c: e.activation(out=szt[:, c, 0:n], in_=pap(szs_[c], 128, n), func=AF.Silu),
                      reads=[pk(szs_[c])], writes=[("sztb", c)])
            if hn_blk is not None:
                H_pe(hn_blk)
                H_evac(hn_blk, hn_l)
            if after_proj is not None:
                after_proj()
            if pre_conv is not None:
                pre_conv()
            if smp:
                n_h = 34 * 7 + DSEQ
                ncol_s = n_h + 28
                nmv_s = ncol_s + 4
                for hf in range(2):
                    base = 272 * hf
                    for c in range(3):
                        for q in range(4):
                            b = 4 * c + q
                            bk = alloc_bank()
                            for j in range(4):
                                S.add("pe", lambda e, c=c, q=q, j=j, bk=bk, base=base: e.matmul(
                                    pbank[bk][32 * j:32 * j + 32, 3 - j:3 - j + nmv_s], ident_bf[:, 32 * q:32 * q + 32],
                                    a_ext[:, c, base:base + nmv_s], start=True, stop=True, tile_position=(0, 32 * j)),
                                    reads=[("aext", c), "ident_bf"], writes=[pk(bk)])
                            copy_op(A4[:, b, 0:ncol_s], pbank[bk][:, 3:3 + ncol_s], [pk(bk)], [("a4", b)])
                    for c in range(3):
                        bk = alloc_bank()
                        for m in range(8):
                            for q in range(4):
                                b = 4 * c + q
                                S.add("pe", lambda e, bk=bk, m=m, q=q, b=b: e.matmul(
                                    pbank[bk][32 * q:32 * q + 32, 0:n_h], W4[:, b, m, :], A4[:, b, 4 * m:4 * m + n_h],
                                    start=(m == 0), stop=(m == 7), tile_position=(0, 32 * q)),
                                    reads=["w4", ("a4", b)], writes=[pk(bk)])
                        src = pbank[bk][:, 0:272].rearrange("p (g r) -> p g r", r=34)[:, :, 0:DSEQ]
                        for dst_t, key, fn_ in ((cbh, "cbh", AF.Identity), (sq, "sq", AF.Square), (cb, "cb", AF.Identity)):
                            dst = dst_t[:, c, hf * 32:(hf + 1) * 32].rearrange("p (g i) -> p g i", i=DSEQ)
                            S.add("act", lambda e, dst=dst, src=src, fn_=fn_, c=c: e.activation(
                                out=dst, in_=src, func=fn_, bias=colsA[:, c, KA:KA + 1]),
                                reads=[pk(bk), "colsA", (key, c)], writes=[(key, c)])
            for c in (range(3) if not smp else ()):
                bk = alloc_bank()
                for m in range(8):
                    for q in range(4):
                        b = 4 * c + q
                        S.add("pe", lambda e, bk=bk, m=m, q=q, b=b: e.matmul(
                            pbank[bk][32 * q:32 * q + 32, 0:n], W4[:, b, m, :], A4[:, b, 4 * m:4 * m + n],
                            start=(m == 0), stop=(m == 7), tile_position=(0, 32 * q)),
                            reads=["w4", ("a4", b)], writes=[pk(bk)])
                if not blk.last:
                    S.add("dve", lambda e, c=c: e.tensor_copy(a_ext[:, c, 0:KA - 1], a_ext[:, c, n:n + KA - 1]),
                          reads=[("aext", c)], writes=[("aext", c)])
                S.add("act", lambda e, c=c, bk=bk: e.activation(out=cbh[:, c, 0:n], in_=pbank[bk][:, 0:n],
                                                                func=AF.Identity, bias=colsA[:, c, KA:KA + 1]),
                      reads=[pk(bk), "colsA"], writes=[("cbh", c)])
                S.add("act", lambda e, c=c, bk=bk: e.activation(out=sq[:, c, 0:n], in_=pbank[bk][:, 0:n],
                                                                func=AF.Square, bias=colsA[:, c, KA:KA + 1]),
                      reads=[pk(bk), "colsA"], writes=[("sq", c)])
                S.add("dve", lambda e, c=c, bk=bk: e.tensor_scalar(cb[:, c, 0:n], pbank[bk][:, 0:n],
                                                                   colsA[:, c, KA:KA + 1], None, ALU.add),
                      reads=[pk(bk), "colsA"], writes=[("cb", c)])
            def state_out():
                if blk.last:
                    so = alloc_bank()
                    for c in range(3):
                        pe_transpose(pfull(so, KA - 1, 512)[:, c * 128:(c + 1) * 128], a32[:, c, 0:KA - 1], ident_f[:],
                                     [("a32", c), "ident_f"], [pk(so)])
                    stg = tC[:].rearrange("p a b -> p (a b)")
                    copy_op(stg[0:KA - 1, 0:WA], pfull(so, KA - 1, WA), [pk(so)], [("tC", 0), ("tC", 1)],
                            scale=0.5, eng="act")
                    S.add("sp", lambda e: e.dma_start(out=sa_p[l], in_=stg[0:KA - 1, 0:WA]),
                          reads=[("tC", 0), ("tC", 1)], writes=[("sa_p", l)], dma=True)
                    so2 = alloc_bank()
                    for c in range(3):
                        pe_transpose(pfull(so2, 2, 512)[:, c * 128:(c + 1) * 128], z32[:, c, 0:2], ident_f[:],
                                     [("z32", c), "ident_f"], [pk(so2)])
                    stg2 = szc[:].rearrange("p a b -> p (a b)")
                    copy_op(stg2[0:2, 0:WA], pfull(so2, 2, WA), [pk(so2)], [("szc", 0), ("szc", 1)], eng="act")
                    S.add("sp", lambda e: e.dma_start(out=sb_p[l], in_=stg2[0:2, 0:WA]),
                          reads=[("szc", 0), ("szc", 1)], writes=[("sb_p", l)], dma=True)
                if smp:
                    stg = tC[:].rearrange("p a b -> p (a b)")
                    for i in range(DSEQ):
                        so = alloc_bank()
                        for c in range(3):
                            pe_transpose(pfull(so, NSEG, 512)[:, c * 128:(c + 1) * 128],
                                         a32[:, c, 0:NS].rearrange("p (g i) -> p g i", i=DSEQ)[:, :, i], ident_f[:],
                                         [("a32", c), "ident_f"], [pk(so)])
                        copy_op(stg[0:NSEG, 0:WA], pfull(so, NSEG, WA), [pk(so)], [("tC", 0), ("tC", 1)],
                                scale=0.5, eng="act")
                        S.add("sp", lambda e, i=i: e.dma_start(out=sa_s[l, :, KA - 1 - DSEQ + i, :], in_=stg[0:NSEG, 0:WA]),
                              reads=[("tC", 0), ("tC", 1)], writes=[("sa_s", l, i)], dma=True)
                    so2 = alloc_bank()
                    for c in range(3):
                        pe_transpose(pfull(so2, 32, 512)[:, c * 128:(c + 1) * 128], z32[:, c, 0:32], ident_f[:],
                                     [("z32", c), "ident_f"], [pk(so2)])
                    stg2 = szc[:].rearrange("p a b -> p (a b)")
                    copy_op(stg2[0:32, 0:WA], pfull(so2, 32, WA), [pk(so2)], [("szc", 0), ("szc", 1)], eng="act")
                    S.add("sp", lambda e: e.dma_start(out=sb_s[l].rearrange("g r c -> (g r) c"), in_=stg2[0:32, 0:WA]),
                          reads=[("szc", 0), ("szc", 1)], writes=[("sb_s", l)], dma=True)

            if blk.last:
                state_out()
            bks = alloc_bank()
            s1 = (bks, 0)
            for c in range(3):
                S.add("pe", lambda e, c=c: e.matmul(pap(s1, 128, n), ones_bf[:], cbh[:, c, 0:n],
                                                    start=(c == 0), stop=(c == 2)),
                      reads=["ones_bf", ("cbh", c)], writes=[pk(s1)])
            s2 = (bks, 1)
            for c in range(3):
                S.add("pe", lambda e, c=c: e.matmul(pap(s2, 128, n), ones_bf[:], sq[:, c, 0:n],
                                                    start=(c == 0), stop=(c == 2)),
                      reads=["ones_bf", ("sq", c)], writes=[pk(s2)])
            S.add("act", lambda e: e.activation(out=mean_b[:, 0:n], in_=pap(s1, 128, n), func=AF.Copy, scale=1.0 / WA),
                  reads=[pk(s1)], writes=["mean_b"])
            S.add("dve", lambda e: e.tensor_tensor(out=vb[:, 0:n], in0=mean_b[:, 0:n], in1=mean_b[:, 0:n], op=ALU.mult),
                  reads=["mean_b"], writes=["vb"])
            S.add("dve", lambda e: e.scalar_tensor_tensor(out=vb[:, 0:n], in0=pap(s2, 128, n), scalar=1.0 / WA,
                                                          in1=vb[:, 0:n], op0=ALU.mult, op1=ALU.subtract),
                  reads=[pk(s2), "vb"], writes=["vb"])
            S.add("act", lambda e: e.activation(out=vb[:, 0:n], in_=vb[:, 0:n], func=AF.Sqrt, bias=eps_col[:, :],
                                                scale=1.0),
                  reads=["vb", "eps"], writes=["vb"])
            S.add("dve", lambda e: e.reciprocal(vb[:, 0:n], vb[:, 0:n]), reads=["vb"], writes=["vb"])
            for c in range(3):
                S.add("dve", lambda e, c=c: e.tensor_tensor(out=cb[:, c, 0:n], in0=cb[:, c, 0:n], in1=mean_b[:, 0:n],
                                                            op=ALU.subtract),
                      reads=[("cb", c), "mean_b"], writes=[("cb", c)])
                S.add("dve", lambda e, c=c: e.tensor_tensor(out=cb[:, c, 0:n], in0=cb[:, c, 0:n], in1=vb[:, 0:n],
                                                            op=ALU.mult),
                      reads=[("cb", c), "vb"], writes=[("cb", c)])
                S.add("act", lambda e, c=c: e.activation(out=cb[:, c, 0:n], in_=cb[:, c, 0:n], func=AF.Silu,
                                                         bias=colsA[:, c, KA + 2:KA + 3], scale=colsA[:, c, KA + 1:KA + 2]),
                      reads=[("cb", c), "colsA"], writes=[("cb", c)])
                S.add("dve", lambda e, c=c: e.tensor_tensor(out=mix_fm[:, c, 0:n], in0=cb[:, c, 0:n],
                                                            in1=szt[:, c, 0:n], op=ALU.mult),
                      reads=[("cb", c), ("sztb", c)], writes=[("mix", c)])
            if smp:
                state_out()

        MIX = [("mix", k) for k in range(8)]

        def O_stage(l, blk):
            for j, (slot, r) in enumerate(blk.tiles):
                for hf in range(2):
                    so = alloc_bank()
                    for k in range(8):
                        S.add("pe", lambda e, k=k, so=so, j=j, r=r, hf=hf: e.matmul(
                            pfull(so, r, 512), mix_fm[:, k, j * 128:j * 128 + r],
                            w_out_sb[:, k, hf * 512:(hf + 1) * 512], start=(k == 0), stop=(k == 7)),
                            reads=MIX + [("wout", hf)], writes=[pk(so)])
                    S.add("dve", lambda e, so=so, slot=slot, r=r, hf=hf: e.tensor_tensor(
                        out=x_all[0:r, slot, hf * 512:(hf + 1) * 512], in0=pfull(so, r, 512),
                        in1=x_all[0:r, slot, hf * 512:(hf + 1) * 512], op=ALU.add),
                        reads=[pk(so), ("x", slot)], writes=[("x", slot)])

        def final_stage(blk):
            nt = len(blk.tiles)
            rows = blk.tiles[0][1]
            for j, (slot, r) in enumerate(blk.tiles):
                S.add("act", lambda e, j=j, slot=slot, r=r: e.activation(
                    out=h_tm[0:r, j, :], in_=x_all[0:r, slot, :], func=AF.Square, accum_out=ss[0:r, j:j + 1]),
                    reads=[("x", slot)], writes=[("htm", j), "ss"], multi=True)
            rstd_from_sumsq(rows, nt)
            for j, (slot, r) in enumerate(blk.tiles):
                S.add("dve", lambda e, j=j, slot=slot, r=r: e.scalar_tensor_tensor(
                    out=x_all[0:r, slot, :], in0=x_all[0:r, slot, :], scalar=rstd[0:r, j:j + 1], in1=fg_b[0:r, :],
                    op0=ALU.mult, op1=ALU.mult),
                    reads=[("x", slot), "rstd", "fg_b"], writes=[("x", slot)])
                if blk.sample:
                    S.add("sp", lambda e, slot=slot, r=r: e.dma_start(out=y_s[:, :], in_=x_all[0:r, slot, :]),
                          reads=[("x", slot)], writes=[("y", slot)], dma=True)
                else:
                    S.add("sp", lambda e, slot=slot, r=r: e.dma_start(out=y_p[slot * 128:(slot + 1) * 128, :],
                                                                      in_=x_all[0:r, slot, :]),
                          reads=[("x", slot)], writes=[("y", slot)], dma=True)

        eps_col = sb("eps_col", [128, 1])
        S.add("pool", lambda e: e.memset(eps_col[:], EPS), writes=["eps"])

        blocks = []
        for b in range(npb):
            blocks.append(Blk(NB, [(2 * b, 128), (2 * b + 1, 128)], False, b == 0, b == npb - 1, b))
        blocks.append(Blk(NS, [(16, NS)], True, False, False, -1))

        for l in range(depth):
            layer_setup_A(l)
            if l == 0:
                load_x(4, 17, q_="pool")
            if l == 0:
                layer_setup_Bdma(l)
            if l == 0:
                H_elem(blocks[0])
                H_pe(blocks[0])
                H_evac(blocks[0], l)
            pre_conv = layer_setup_Bparams(l)
            main_early(l, blocks[0])
            pre_conv()
            pre_conv = None
            layer_setup_Bcomp(l)
            if l + 1 < depth:
                weight_prestage(l + 1)
            for bi, blk in enumerate(blocks):
                nxt = blocks[bi + 1] if bi + 1 < len(blocks) else None
                refill = (nxt is None and l + 1 < depth)
                main_late(l, blk, nxt, pre_conv if bi == 0 else None,
                          (lambda: weight_loads(l + 1, ("late",))) if refill else None,
                          (blocks[0], l + 1) if refill else None)
                if nxt is not None:
                    if nxt.last:
                        sample_hist_a_dma(l)
                    if nxt.sample:
                        sample_hist_z(l)
                    main_early(l, nxt)
                    if nxt.last:
                        sample_hist_a_pe(l)
                        sample_hist_z_dma(l)
                    if bi + 2 == len(blocks) and l + 1 < depth:
                        weight_loads(l + 1, ("early",))
                if refill:
                    layer_setup_Bdma(l + 1, "pool")
                O_stage(l, blk)
                if refill:
                    weight_loads(l + 1, ("wout",))
                if l == depth - 1:
                    final_stage(blk)

        with nc.Block() as block:
            S.emit(nc, stack, block)
    return nc


_CACHE = {}


def _get_program():
    if "nc" not in _CACHE:
        _CACHE["nc"] = build_program()
    return _CACHE["nc"]


def kernel(x_prompt, x_sample, state_a_conv, state_b_conv, norm_g, w_in, a_conv_w, a_conv_b, a_ln_g, a_ln_b,
           b_conv_w, c_ln_g, c_ln_b, c_ws, c_bs, w_out, final_g):
    f = lambda a: np.ascontiguousarray(np.asarray(a, dtype=np.float32))
    x_prompt, x_sample = f(x_prompt), f(x_sample)
    state_a_conv, state_b_conv = f(state_a_conv), f(state_b_conv)
    shared = {
        "norm_g": f(norm_g).reshape(DEPTH, 8, 128),
        "w_in": f(w_in),
        "a_conv_w": f(a_conv_w),
        "a_conv_b": f(a_conv_b).reshape(DEPTH, 1, WA),
        "a_ln_g": f(a_ln_g).reshape(DEPTH, 1, WA),
        "a_ln_b": f(a_ln_b).reshape(DEPTH, 1, WA),
        "b_conv_w": f(b_conv_w),
        "c_ln_g": f(c_ln_g).reshape(DEPTH, 1, WC),
        "c_ln_b": f(c_ln_b).reshape(DEPTH, 1, WC),
        "c_ws": f(c_ws),
        "c_bs": f(c_bs),
        "w_out": f(w_out),
        "final_g": f(final_g).reshape(1, D),
    }
    in_maps = []
    for i in range(NCORES):
        m = dict(shared)
        m["x_p"] = x_prompt[i]
        m["x_s"] = np.ascontiguousarray(x_sample[NSEG * i:NSEG * (i + 1)].reshape(NS, D))
        m["st_a"] = np.ascontiguousarray(state_a_conv[:, NSEG * i:NSEG * (i + 1)])
        m["st_b"] = np.ascontiguousarray(state_b_conv[:, NSEG * i:NSEG * (i + 1)])
        in_maps.append(m)
    nc = _get_program()
    res = run_bass_kernel_spmd(nc, in_maps, core_ids=list(range(NCORES)))
    R = res.results
    y_prompt = np.stack([R[i]["y_p"] for i in range(NCORES)], axis=0)
    y_sample = np.concatenate([R[i]["y_s"].reshape(NSEG, DSEQ, D) for i in range(NCORES)], axis=0)
    sa_p = np.stack([R[i]["sa_p"] for i in range(NCORES)], axis=1)
    sb_p = np.stack([R[i]["sb_p"] for i in range(NCORES)], axis=1)
    sa_s = np.concatenate([R[i]["sa_s"] for i in range(NCORES)], axis=1)
    sb_s = np.concatenate([R[i]["sb_s"] for i in range(NCORES)], axis=1)
    sv_s = np.concatenate([R[i]["sv_s"] for i in range(NCORES)], axis=1)
    out = (y_prompt, y_sample, sa_p, sb_p, sa_s, sb_s, sv_s)
    return tuple(np.ascontiguousarray(o, dtype=np.float32) for o in out)
```

```python
import numpy as np
from contextlib import ExitStack

import concourse.bass as bass
import concourse.mybir as mybir
from concourse.bass_utils import run_bass_kernel_spmd

F32 = mybir.dt.float32
BF16 = mybir.dt.bfloat16
AF = mybir.ActivationFunctionType
ALU = mybir.AluOpType

NCORES = 8
D = 1024
INW = 3456
WA = 384
WC = 256
DEPTH = 2
SEQ = 2048
NSEG = 16
DSEQ = 4
NS = NSEG * DSEQ
KA = 31
KB = 3
NB = 256
NPB = SEQ // NB
EPS = 1e-6
C_AV, C_AG, C_AZ = 0, 384, 768
C_BH, C_BB, C_BC, C_BZ = 1152, 1536, 1920, 2304
C_CU, C_CV, C_CZ = 2688, 2944, 3200
PIECES = [(C_AV, 384), (C_AG, 384), (C_BH, 384), (C_BC, 384), (C_CV, 256), (C_BB, 384), (C_BZ, 384),
          (C_CU, 256), (C_CZ, 256), (C_AZ, 384)]


def piece_of(col):
    for i, (c0, w) in enumerate(PIECES):
        if c0 <= col < c0 + w:
            return i
    raise ValueError(col)


class Op:
    __slots__ = ("eng", "fn", "deps", "multi", "dma", "idx", "sig", "prewait")

    def __init__(self, eng, fn, multi, dma, idx):
        self.eng, self.fn, self.multi, self.dma, self.idx = eng, fn, multi, dma, idx
        self.deps = set()
        self.sig = None
        self.prewait = None


class Sched:
    COMPUTE = ("pe", "act", "dve", "pool")
    MAXV = 4000
    NDMA = {"sp": 20, "pool": 8, "act": 4}

    def __init__(self):
        self.ops = []
        self.lastw = {}
        self.rd = {}

    def add(self, eng, fn, reads=(), writes=(), multi=False, dma=False):
        op = Op(eng, fn, multi, dma, len(self.ops))
        psr = [k for k in reads if isinstance(k, tuple) and k[0] == "psb"]
        if psr:
            reads = [k for k in reads if not (isinstance(k, tuple) and k[0] == "psb")]
            writes = list(writes) + psr
        for k in reads:
            w = self.lastw.get(k)
            if w is not None:
                op.deps.add(w)
        for k in writes:
            w = self.lastw.get(k)
            if w is not None:
                op.deps.add(w)
            op.deps.update(self.rd.get(k, ()))
        for k in reads:
            self.rd.setdefault(k, []).append(op.idx)
        for k in writes:
            self.lastw[k] = op.idx
            self.rd[k] = []
        op.deps.discard(op.idx)
        self.ops.append(op)
        return op

    def emit(self, nc, stack, block):
        ops = self.ops
        for op in ops:
            latest = {}
            keep = set()
            for p in op.deps:
                po = ops[p]
                if po.dma:
                    keep.add(p)
                    continue
                if po.eng == "pe" and op.eng == "pe" and not op.dma:
                    continue
                if po.eng not in latest or latest[po.eng] < p:
                    latest[po.eng] = p
            keep.update(latest.values())
            op.deps = keep
        needed = set()
        for op in ops:
            needed.update(op.deps)
        queues = {}
        for op in ops:
            queues.setdefault(op.eng, []).append(op)
        sems = {}

        def getsem(name):
            if name not in sems:
                sems[name] = stack.enter_context(nc.semaphore(name))
            return sems[name]

        for eng in self.COMPUTE:
            cnt = 0
            for op in queues.get(eng, []):
                if op.dma:
                    continue
                if op.idx in needed:
                    e, v = divmod(cnt, self.MAXV)
                    getsem("c_%s_%d" % (eng, e))
                    op.sig = ("c_%s_%d" % (eng, e), v + 1, 1)
                    cnt += 1
        all_dma_final = {}
        for eng, q in queues.items():
            n = self.NDMA.get(eng, 4)
            uses = [0] * n
            rr = 0
            for op in q:
                if not op.dma:
                    continue
                s = "d_%s_%d" % (eng, rr)
                getsem(s)
                if uses[rr] > 0:
                    op.prewait = (s, 16 * uses[rr])
                uses[rr] += 1
                op.sig = (s, 16 * uses[rr], 16)
                all_dma_final[s] = 16 * uses[rr]
                rr = (rr + 1) % n
        handles = {"pe": block.tensor, "act": block.scalar, "dve": block.vector, "pool": block.gpsimd,
                   "sp": block.sync}
        order = ["sp", "pool", "act", "dve", "pe"]
        for eng in order:
            q = queues.get(eng, [])

            def body(e, q=q, eng=eng):
                seen = {}
                for op in q:
                    waits = {}
                    for p in op.deps:
                        s, v, _ = ops[p].sig
                        if waits.get(s, 0) < v:
                            waits[s] = v
                    if op.prewait is not None:
                        s, v = op.prewait
                        if waits.get(s, 0) < v:
                            waits[s] = v
                    wl = [(s, v) for s, v in waits.items() if seen.get(s, 0) < v]
                    for s, v in wl:
                        seen[s] = v
                    attach = None
                    if wl and not op.multi:
                        attach = wl.pop()
                    for s, v in wl:
                        e.wait_ge(sems[s], v)
                    ins = op.fn(e)
                    if attach is not None:
                        ins._wait_ge(sems[attach[0]], attach[1])
                    if op.sig is not None:
                        ins.then_inc(sems[op.sig[0]], op.sig[2])
                if eng == "sp":
                    for s, v in all_dma_final.items():
                        if seen.get(s, 0) < v:
                            e.wait_ge(sems[s], v)

            handles[eng](body)


class Blk:
    def __init__(self, ntok, tiles, sample, first, last, pidx):
        self.ntok, self.tiles, self.sample, self.first, self.last, self.pidx = ntok, tiles, sample, first, last, pidx


def build_program(depth=DEPTH, npb=NPB):
    nc = bass.Bass("TRN2", target_bir_lowering=False)
    S = Sched()

    def din(name, shape):
        return nc.dram_tensor(name, shape, F32, kind="ExternalInput").ap()

    def dout(name, shape):
        return nc.dram_tensor(name, shape, F32, kind="ExternalOutput").ap()

    x_p = din("x_p", [npb * NB, D])
    x_s = din("x_s", [NS, D])
    st_a = din("st_a", [DEPTH, NSEG, KA - 1, WA])
    st_b = din("st_b", [DEPTH, NSEG, KB - 1, WA])
    norm_g = din("norm_g", [DEPTH, 8, 128])
    w_in = din("w_in", [DEPTH, D, INW])
    a_conv_w = din("a_conv_w", [DEPTH, KA, WA])
    a_conv_b = din("a_conv_b", [DEPTH, 1, WA])
    a_ln_g = din("a_ln_g", [DEPTH, 1, WA])
    a_ln_b = din("a_ln_b", [DEPTH, 1, WA])
    b_conv_w = din("b_conv_w", [DEPTH, KB, WA])
    c_ln_g = din("c_ln_g", [DEPTH, 1, WC])
    c_ln_b = din("c_ln_b", [DEPTH, 1, WC])
    c_ws = din("c_ws", [DEPTH, 4, 128, 128])
    c_bs = din("c_bs", [DEPTH, 4, 128])
    w_out = din("w_out", [DEPTH, D, D])
    final_g = din("final_g", [1, D])

    y_p = dout("y_p", [npb * NB, D])
    y_s = dout("y_s", [NS, D])
    sa_p = dout("sa_p", [DEPTH, KA - 1, WA])
    sb_p = dout("sb_p", [DEPTH, KB - 1, WA])
    sa_s = dout("sa_s", [DEPTH, NSEG, KA - 1, WA])
    sb_s = dout("sb_s", [DEPTH, NSEG, KB - 1, WA])
    sv_s = dout("sv_s", [DEPTH, NSEG, DSEQ, WC])

    w_in_scr = nc.dram_tensor("w_in_scr", [128, 8, INW], BF16).ap()
    w_out_scr = nc.dram_tensor("w_out_scr", [128, 8, D], BF16).ap()
    stack = ExitStack()
    with stack:
        def sb(name, shape, dt=F32):
            return stack.enter_context(nc.sbuf_tensor(name, shape, dt))

        def ps(name, shape, dt=F32):
            return stack.enter_context(nc.psum_tensor(name, shape, dt))

        x_all = sb("x_all", [128, 17, D])
        w_in_sb = sb("w_in_sb", [128, 8, INW], BF16)
        w_out_sb = sb("w_out_sb", [128, 8, D], BF16)
        W4 = sb("W4", [128, 12, 8, 32], BF16)
        A4 = sb("A4", [128, 12, NB + 28], BF16)
        id32 = sb("id32", [128, 32], BF16)
        wcol4 = sb("wcol4", [128, 12, 8])
        colsK = sb("colsK", [128, 3, 4, 8])
        h_tm = sb("h_tm", [128, 2, D], BF16)
        h_fm = sb("h_fm", [128, 8, NB], BF16)
        mix_fm = sb("mix_fm", [128, 8, NB], BF16)
        a_ext = sb("a_ext", [128, 3, NSEG * 34 + 4], BF16)
        z_ext = sb("z_ext", [128, 3, NB + 4], BF16)
        th = sb("th", [128, 2, NB])
        cb = sb("cb", [128, 3, NB])
        cbh = sb("cbh", [128, 3, NB], BF16)
        sq = sb("sq", [128, 3, NB], BF16)
        mean_b = sb("mean_b", [128, NB])
        vb = sb("vb", [128, NB])
        szt = sb("szt", [128, 3, NB])
        bh = sb("bh", [128, 2, NB])
        bb = sb("bb", [128, 2, NB])
        szb = sb("szb", [128, 2, NB])
        ncv = sb("ncv", [128, 2, WC])
        vn_bf = sb("vn_bf", [128, 2, WC], BF16)
        tC = sb("tC", [128, 2, NB])
        szc = sb("szc", [128, 2, NB])
        a32 = sb("a32", [128, 3, NS])
        z32 = sb("z32", [128, 3, 32])
        ident_bf = sb("ident_bf", [128, 128], BF16)
        ident_f = sb("ident_f", [128, 128])
        maskT = sb("maskT", [128, 128])
        maskB = sb("maskB", [64, NSEG, 4])
        ones_bf = sb("ones_bf", [128, 128], BF16)
        wmT = sb("wmT", [128, 4, 128], BF16)
        wblk = sb("wblk", [64, 4, 64], BF16)
        cbias = sb("cbias", [128, 2, 128])
        gC_b = sb("gC_b", [128, WC])
        bC_b = sb("bC_b", [128, WC])
        fg_b = sb("fg_b", [128, D])
        colsA = sb("colsA", [128, 3, 40])
        colsH = sb("colsH", [128, 3, 32])
        g_col = sb("g_col", [128, DEPTH * 8])
        stG_all = sb("stG_all", [DEPTH * 8, 128])
        ss = sb("ss", [128, 2])
        rstd = sb("rstd", [128, 2])
        st6 = sb("st6", [128, 2, 6])
        mv = sb("mv", [128, 2, 2])
        rstdc = sb("rstdc", [128, 2])

        NBANK = 8
        pbank = [ps("pb%d" % i, [128, 512]) for i in range(NBANK)]
        hT_state = {}

        state = {"bank": 0}

        def alloc_bank():
            b = state["bank"]
            state["bank"] = (b + 1) % NBANK
            return b

        def pap(bh_, rows=128, n=NB, p0=0):
            b, hf = bh_
            return pbank[b][p0:p0 + rows, hf * 256:hf * 256 + n]

        def pfull(b, rows=128, n=512):
            return pbank[b][0:rows, 0:n]

        def pk(x):
            return ("psb", x[0] if isinstance(x, tuple) else x)

        ones_f = cb[:, 0, 0:128]
        S.add("pool", lambda e: e.memset(ones_f, 1.0), writes=[("cb", 0)])
        S.add("pool", lambda e: e.memset(ones_bf[:], 1.0), writes=["ones_bf"])
        S.add("pool", lambda e: e.affine_select(out=ident_f[:], in_=ones_f, pattern=[[-1, 128]],
                                                compare_op=ALU.is_equal, fill=0.0, base=0, channel_multiplier=1),
              reads=[("cb", 0)], writes=["ident_f"])
        S.add("pool", lambda e: e.affine_select(out=maskT[:], in_=ones_f, pattern=[[-1, 128]],
                                                compare_op=ALU.is_ge, fill=0.0, base=0, channel_multiplier=1),
              reads=[("cb", 0)], writes=["maskT"])
        S.add("pool", lambda e: e.affine_select(out=maskB[:], in_=ones_f[0:64, 0:64].rearrange("p (g s) -> p g s", s=4),
                                                pattern=[[-4, NSEG], [-1, 4]],
                                                compare_op=ALU.is_ge, fill=0.0, base=0, channel_multiplier=1),
              reads=[("cb", 0)], writes=["maskB"])
        S.add("pool", lambda e: e.affine_select(out=maskB[:], in_=maskB[:], pattern=[[4, NSEG], [0, 4]],
                                                compare_op=ALU.is_ge, fill=0.0, base=3, channel_multiplier=-1),
              reads=["maskB"], writes=["maskB"])
        S.add("pool", lambda e: e.tensor_copy(ident_bf[:], ident_f[:]), reads=["ident_f"], writes=["ident_bf"])
        S.add("pool", lambda e: e.tensor_tensor(out=id32[:], in0=ident_bf[:, 0:32], in1=ident_bf[:, 32:64], op=ALU.add),
              reads=["ident_bf"], writes=["id32"])
        S.add("pool", lambda e: e.tensor_tensor(out=id32[:], in0=id32[:], in1=ident_bf[:, 64:96], op=ALU.add),
              reads=["ident_bf", "id32"], writes=["id32"])
        S.add("pool", lambda e: e.tensor_tensor(out=id32[:], in0=id32[:], in1=ident_bf[:, 96:128], op=ALU.add),
              reads=["ident_bf", "id32"], writes=["id32"])
        S.add("pool", lambda e: e.memset(colsH[:], 0.0), writes=["colsH"])
        S.add("sp", lambda e: e.dma_start(out=stG_all[:], in_=norm_g.rearrange("l k p -> (l k) p")),
              writes=["stG_all"], dma=True)
        S.add("pool", lambda e: e.memset(A4[:], 0.0), writes=[("a4", b) for b in range(12)])
        S.add("pool", lambda e: e.memset(a_ext[:], 0.0), writes=[("aext", 0), ("aext", 1), ("aext", 2)])
        S.add("sp", lambda e: e.dma_start(out=fg_b[:], in_=bass.AP(final_g.tensor, 0, [[0, 128], [1, D]])),
              writes=["fg_b"], dma=True)
        def load_x(j0, j1, q_="sp"):
            for j in range(j0, j1):
                if j < 2 * npb:
                    S.add(q_, lambda e, j=j: e.dma_start(out=x_all[:, j, :], in_=x_p[j * 128:(j + 1) * 128, :]),
                          writes=[("x", j)], dma=True)
            if j1 > 2 * npb:
                S.add(q_, lambda e: e.dma_start(out=x_all[0:NS, 16, :], in_=x_s[:, :]),
                      writes=[("x", 16)], dma=True)

        load_x(0, 4)

        cp_state = {"i": 0}

        def copy_op(out, in_, reads, writes, scale=None, eng=None):
            if eng is None:
                eng = "act" if cp_state["i"] % 2 == 0 else "dve"
                cp_state["i"] += 1
            if eng == "act":
                if scale is None:
                    S.add("act", lambda e: e.activation(out=out, in_=in_, func=AF.Copy), reads=reads, writes=writes)
                else:
                    S.add("act", lambda e: e.activation(out=out, in_=in_, func=AF.Copy, scale=scale),
                          reads=reads, writes=writes)
            else:
                if scale is None:
                    S.add("dve", lambda e: e.tensor_copy(out, in_), reads=reads, writes=writes)
                else:
                    S.add("dve", lambda e: e.tensor_scalar(out, in_, scale, None, ALU.mult),
                          reads=reads, writes=writes)

        def pe_transpose(out, in_, ident, reads, writes):
            S.add("pe", lambda e: e.transpose(out, in_, ident), reads=reads, writes=writes)

        _sg = (alloc_bank(), 0)
        pe_transpose(pap(_sg, 128, DEPTH * 8), stG_all[:], ident_f[0:DEPTH * 8, 0:DEPTH * 8], ["stG_all", "ident_f"],
                     [pk(_sg)])
        copy_op(g_col[:, :], pap(_sg, 128, DEPTH * 8), [pk(_sg)], ["g_col"], eng="act")

        K_TH = [("thb", 0), ("thb", 1)]
        K_CB = [("cb", 0), ("cb", 1), ("cb", 2)]
        K_SZT = [("sztb", 0), ("sztb", 1), ("sztb", 2)]
        K_BH = [("bhb", 0), ("bhb", 1)]
        K_BB = [("bbb", 0), ("bbb", 1)]
        K_SZB = [("szbb", 0), ("szbb", 1)]

        def weight_loads(l, which=("early", "late", "wout")):
            wkeys = []
            if l == 0:
                w_in_v = w_in[l].rearrange("(k p) e -> p k e", p=128)
                w_out_v = w_out[l].rearrange("(k p) e -> p k e", p=128)
            else:
                w_in_v, w_out_v = w_in_scr, w_out_scr
            for pi, (c0, w) in enumerate(PIECES):
                if ("early" if pi < 5 else "late") not in which:
                    continue
                S.add("pool", lambda e, c0=c0, w=w: e.dma_start(out=w_in_sb[:, :, c0:c0 + w], in_=w_in_v[:, :, c0:c0 + w]),
                      reads=(wkeys[-3:-2] if l == 0 else [("wscr", pi)]), writes=[("win", pi)], dma=True)
                wkeys.append(("win", pi))
            if "wout" in which:
                for hf in range(2):
                    S.add("pool", lambda e, hf=hf: e.dma_start(out=w_out_sb[:, :, hf * 512:(hf + 1) * 512],
                                                               in_=w_out_v[:, :, hf * 512:(hf + 1) * 512]),
                          reads=(wkeys[-3:-2] if l == 0 else [("wscr_o", hf)]), writes=[("wout", hf)], dma=True)
                    wkeys.append(("wout", hf))

        def weight_prestage(l):
            w_in_v = w_in[l].rearrange("(k p) e -> p k e", p=128)
            w_out_v = w_out[l].rearrange("(k p) e -> p k e", p=128)
            for pi, (c0, w) in enumerate(PIECES):
                S.add("pool", lambda e, c0=c0, w=w: e.dma_start(out=w_in_scr[:, :, c0:c0 + w],
                                                                in_=w_in_v[:, :, c0:c0 + w]),
                      writes=[("wscr", pi)], dma=True)
            for hf in range(2):
                S.add("pool", lambda e, hf=hf: e.dma_start(out=w_out_scr[:, :, hf * 512:(hf + 1) * 512],
                                                           in_=w_out_v[:, :, hf * 512:(hf + 1) * 512]),
                      writes=[("wscr_o", hf)], dma=True)

        def layer_setup_A(l):
            if l == 0:
                weight_loads(l)
            S.add("sp", lambda e: e.dma_start(out=gC_b[:], in_=bass.AP(c_ln_g.tensor, l * WC, [[0, 128], [1, WC]])),
                  writes=["gC_b"], dma=True)
            S.add("sp", lambda e: e.dma_start(out=bC_b[:], in_=bass.AP(c_ln_b.tensor, l * WC, [[0, 128], [1, WC]])),
                  writes=["bC_b"], dma=True)
            S.add("dve", lambda e: e.memset(a_ext[:, :, 0:KA - 1], 0.0), writes=[("aext", 0), ("aext", 1), ("aext", 2)])
            S.add("dve", lambda e: e.memset(z_ext[:, :, 0:KB - 1], 0.0), writes=[("zext", 0), ("zext", 1), ("zext", 2)])

        def layer_setup_Bdma(l, q_="sp"):
            thf = cb[:].rearrange("p a b -> p (a b)")
            sztf = szt[:].rearrange("p a b -> p (a b)")
            stP = thf[0:37, 0:WA]
            srcs = [(a_conv_w[l], 0, KA), (a_conv_b[l], KA, 1), (a_ln_g[l], KA + 1, 1), (a_ln_b[l], KA + 2, 1),
                    (b_conv_w[l], KA + 3, KB)]
            for i_, (src, r0, nr) in enumerate(srcs):
                S.add(q_, lambda e, src=src, r0=r0, nr=nr: e.dma_start(out=thf[r0:r0 + nr, 0:WA], in_=src),
                      writes=(K_CB if i_ == 0 else []) + [("stP", i_)], dma=True)
            wsn = sztf[:, 0:512].rearrange("p (h s) -> p h s", h=4)
            S.add(q_, lambda e: e.dma_start(out=wsn, in_=c_ws[l].rearrange("h t s -> t h s")),
                  writes=K_SZT, dma=True)
            wrc = vb[0:64, 128:144].rearrange("p (h s) -> p h s", h=4)
            for g in range(NSEG):
                src = bass.AP(c_ws.tensor, l * 4 * 128 * 128, [[128, 4], [128 * 128, 4], [1, 4]])
                S.add(q_, lambda e, g=g, src=src: e.dma_start(out=wrc[4 * g:4 * g + 4], in_=src),
                      writes=(["vb"] if g == 0 else []) + [("wrc", g)], dma=True)
            for h in range(4):
                src = bass.AP(c_bs.tensor, (l * 4 + h) * 128, [[0, 64], [1, 128]])
                S.add(q_, lambda e, h=h, src=src: e.dma_start(out=cbias[(h % 2) * 64:(h % 2) * 64 + 64, h // 2, :],
                                                              in_=src), writes=[("cbias", h)], dma=True)
            S.add(q_, lambda e: e.dma_start(out=sa_s[l, :, 0:KA - 1 - DSEQ, :], in_=st_a[l, :, DSEQ:KA - 1, :]),
                  writes=[("sa_s_copy", l)], dma=True)

        def layer_setup_Bparams(l):
            thf = cb[:].rearrange("p a b -> p (a b)")
            stP = thf[0:37, 0:WA]
            for c in range(3):
                s = (alloc_bank(), 0)
                pe_transpose(pap(s, 128, 37), stP[:, c * 128:(c + 1) * 128], ident_f[0:37, 0:37],
                             K_CB + [("stP", i_) for i_ in range(5)] + ["ident_f"], [pk(s)])
                copy_op(colsA[:, c, 0:37], pap(s, 128, 37), [pk(s)], ["colsA"], eng="act")
            S.add("dve", lambda e: e.tensor_scalar(colsH[:, :, 0:KA], colsA[:, :, 0:KA], 0.5, None, ALU.mult),
                  reads=["colsA"], writes=["colsH"])
            S.add("dve", lambda e: e.tensor_copy(colsK[:], colsH[:, :, 0:32].rearrange("p c (m j) -> p c j m", j=4)),
                  reads=["colsH"], writes=["colsK"])
            for q in range(4):
                for j in range(4):
                    S.add("sp", lambda e, q=q, j=j: e.dma_start(
                        out=wcol4[32 * j:32 * j + 32, :, :].rearrange("p (c q) m -> p c q m", q=4)[:, :, q, :],
                        in_=colsK[32 * q:32 * q + 32, :, j, :]),
                        reads=["colsK"], writes=[("wcol4", j, q)], dma=True)
            sztf_ = szt[:].rearrange("p a b -> p (a b)")
            wsn = sztf_[:, 0:512].rearrange("p (h s) -> p h s", h=4)
            wrc = vb[0:64, 128:144].rearrange("p (h s) -> p h s", h=4)
            wr = mean_b[0:64, :].rearrange("p (h g s) -> p h g s", h=4, g=NSEG)
            S.add("dve", lambda e: e.tensor_tensor(out=wsn, in0=wsn, in1=maskT[:].unsqueeze(1).broadcast_to([128, 4, 128]),
                                                   op=ALU.mult),
                  reads=K_SZT + ["maskT"], writes=K_SZT)
            S.add("dve", lambda e: e.tensor_tensor(out=wr, in0=wrc.unsqueeze(2).broadcast_to([64, 4, NSEG, 4]),
                                                   in1=maskB[:].unsqueeze(1).broadcast_to([64, 4, NSEG, 4]), op=ALU.mult),
                  reads=["vb", "maskB"] + [("wrc", g) for g in range(NSEG)], writes=["mean_b"])
            def w4_build():
                WC4 = [("wcol4", j, q) for j in range(4) for q in range(4)]
                S.add("dve", lambda e: e.tensor_tensor(
                    out=W4[:].rearrange("p b m c -> p (b m) c"),
                    in0=id32[:].unsqueeze(1).broadcast_to([128, 96, 32]),
                    in1=wcol4[:].rearrange("p b m -> p (b m)").unsqueeze(2).broadcast_to([128, 96, 32]), op=ALU.mult),
                    reads=WC4 + ["id32"], writes=["w4"])
            return w4_build

        def layer_setup_Bcomp(l):
            thf = cb[:].rearrange("p a b -> p (a b)")
            sztf = szt[:].rearrange("p a b -> p (a b)")
            stP = thf[0:37, 0:WA]
            wsn = sztf[:, 0:512].rearrange("p (h s) -> p h s", h=4)
            wrc = vb[0:64, 128:144].rearrange("p (h s) -> p h s", h=4)
            for h in range(4):
                s = (alloc_bank(), 0)
                pe_transpose(pap(s, 128, 128), wsn[:, h, :], ident_f[:], K_SZT + ["ident_f"], [pk(s)])
                copy_op(wmT[:, h, :], pap(s, 128, 128), [pk(s)], ["wmT"])
            wr = mean_b[0:64, :].rearrange("p (h g s) -> p h g s", h=4, g=NSEG)
            wr3 = mean_b[0:64, :].rearrange("p (h q) -> p h q", h=4)
            for h in range(4):
                s = (alloc_bank(), 0)
                pe_transpose(pap(s, 64, 64), wr3[:, h, :], ident_f[0:64, 0:64], ["mean_b", "ident_f"], [pk(s)])
                copy_op(wblk[:, h, :], pap(s, 64, 64), [pk(s)], ["wblk"])

        def _hist_stages():
            return [(szb[:].rearrange("p a b -> p (a b)"), K_SZB),
                    (tC[:].rearrange("p a b -> p (a b)"), [("tC", 0), ("tC", 1)]),
                    (szc[:].rearrange("p a b -> p (a b)"), [("szc", 0), ("szc", 1)]),
                    (bb[:].rearrange("p a b -> p (a b)"), K_BB)]

        def sample_hist_a_dma(l):
            for q, (stg, key) in enumerate(_hist_stages()):
                src = st_a[l, 4 * q:4 * q + 4].rearrange("g r c -> (g r) c")
                S.add("sp", lambda e, stg=stg, src=src: e.dma_start(out=stg[0:120, 0:WA], in_=src),
                      writes=key, dma=True)

        def sample_hist_a_pe(l):
            aS = a_ext[:, :, 0:NSEG * 34].rearrange("p c (g r) -> p c g r", r=34)
            for q, (stg, key) in enumerate(_hist_stages()):
                for c in range(3):
                    s = (alloc_bank(), 0)
                    pe_transpose(pap(s, 128, 120), stg[0:120, c * 128:(c + 1) * 128], ident_f[0:120, 0:120],
                                 key + ["ident_f"], [pk(s)])
                    copy_op(aS[:, c, 4 * q:4 * q + 4, 0:KA - 1],
                            pap(s, 128, 120).rearrange("p (g r) -> p g r", r=KA - 1), [pk(s)], [("aext", c)],
                            scale=2.0)

        K_NCV = [("ncv", 0), ("ncv", 1)]

        def sample_hist_z_dma(l):
            ncvf = ncv[:].rearrange("p a b -> p (a b)")
            S.add("sp", lambda e: e.dma_start(out=ncvf[0:32, 0:WA], in_=st_b[l].rearrange("g r c -> (g r) c")),
                  writes=K_NCV, dma=True)

        def sample_hist_z(l):
            ncvf = ncv[:].rearrange("p a b -> p (a b)")
            zS = z_ext[:, :, 0:NSEG * 6].rearrange("p c (g r) -> p c g r", r=6)
            for c in range(3):
                s = (alloc_bank(), 0)
                pe_transpose(pap(s, 128, 32), ncvf[0:32, c * 128:(c + 1) * 128], ident_f[0:32, 0:32],
                             K_NCV + ["ident_f"], [pk(s)])
                copy_op(zS[:, c, :, 0:KB - 1], pap(s, 128, 32).rearrange("p (g r) -> p g r", r=KB - 1),
                        [pk(s)], [("zext", c)])

        def rstd_from_sumsq(rows, nt):
            S.add("act", lambda e: e.activation(out=rstd[0:rows, 0:nt], in_=ss[0:rows, 0:nt], func=AF.Sqrt,
                                                bias=eps_col[0:rows, :], scale=1.0 / D),
                  reads=["ss", "eps"], writes=["rstd"])
            S.add("dve", lambda e: e.reciprocal(rstd[0:rows, 0:nt], rstd[0:rows, 0:nt]),
                  reads=["rstd"], writes=["rstd"])

        def H_elem(blk):
            nt = len(blk.tiles)
            rows = blk.tiles[0][1]
            for j, (slot, r) in enumerate(blk.tiles):
                S.add("act", lambda e, j=j, slot=slot, r=r: e.activation(
                    out=h_tm[0:r, j, :], in_=x_all[0:r, slot, :], func=AF.Square, accum_out=ss[0:r, j:j + 1]),
                    reads=[("x", slot)], writes=[("htm", j), "ss"], multi=True)
            rstd_from_sumsq(rows, nt)
            for j, (slot, r) in enumerate(blk.tiles):
                S.add("dve", lambda e, j=j, slot=slot, r=r: e.tensor_scalar(
                    h_tm[0:r, j, :], x_all[0:r, slot, :], rstd[0:r, j:j + 1], None, ALU.mult),
                    reads=[("x", slot), "rstd"], writes=[("htm", j)])

        def hT_view(bank):
            return pbank[bank][:].bitcast(BF16).rearrange("p (k t) -> p k t", t=NB)

        def H_pe(blk):
            banks = (alloc_bank(), alloc_bank())
            hT_state["banks"] = banks
            for j, (slot, r) in enumerate(blk.tiles):
                for k in range(8):
                    pe_transpose(hT_view(banks[k // 4])[:, k % 4, j * 128:j * 128 + r], h_tm[0:r, j, k * 128:(k + 1) * 128],
                                 ident_bf[0:r, 0:r], [("htm", j), "ident_bf"], [pk(banks[k // 4])])

        def H_evac(blk, l):
            n = blk.ntok
            banks = hT_state["banks"]
            for k in range(8):
                if k < 4:
                    S.add("act", lambda e, k=k: e.activation(out=h_fm[:, k, 0:n], in_=hT_view(banks[0])[:, k % 4, 0:n],
                                                            func=AF.Copy, scale=g_col[:, l * 8 + k:l * 8 + k + 1]),
                          reads=[pk(banks[0]), "g_col"], writes=[("hfm", k)])
                else:
                    S.add("dve", lambda e, k=k: e.tensor_scalar(h_fm[:, k, 0:n], hT_view(banks[1])[:, k % 4, 0:n],
                                                               g_col[:, l * 8 + k:l * 8 + k + 1], None, ALU.mult),
                          reads=[pk(banks[1]), "g_col"], writes=[("hfm", k)])

        HFM = [("hfm", k) for k in range(8)]

        def proj_fm(col0, n, s):
            pi = piece_of(col0)
            for k in range(8):
                S.add("pe", lambda e, k=k, s=s: e.matmul(pap(s, 128, n), w_in_sb[:, k, col0:col0 + 128],
                                                        h_fm[:, k, 0:n], start=(k == 0), stop=(k == 7)),
                      reads=[("win", pi)] + HFM, writes=[pk(s)])
            return s

        def main_early(l, blk):
            n = blk.ntok
            smp = blk.sample
            if smp:
                aS = a_ext[:, :, 0:NSEG * 34].rearrange("p c (g r) -> p c g r", r=34)
                zS = z_ext[:, :, 0:NSEG * 6].rearrange("p c (g r) -> p c g r", r=6)

                def v3(ap2):
                    return ap2.rearrange("p (g i) -> p g i", i=DSEQ)

                def awin(c, k):
                    return aS[:, c, :, k:k + DSEQ]

                def zwin(c, k):
                    return zS[:, c, :, k:k + DSEQ]
            else:
                def v3(ap2):
                    return ap2

                def awin(c, k):
                    return a_ext[:, c, k:k + n]

                def zwin(c, k):
                    return z_ext[:, c, k:k + n]
            nt = len(blk.tiles)
            rows = blk.tiles[0][1]

            for c in range(3):
                bk = alloc_bank()
                sv = proj_fm(C_AV + c * 128, n, (bk, 0))
                sg = proj_fm(C_AG + c * 128, n, (bk, 1))
                tb = c % 2
                S.add("act", lambda e, sg=sg, tb=tb: e.activation(out=th[:, tb, 0:n], in_=pap(sg, 128, n),
                                                                  func=AF.Tanh, scale=0.5),
                      reads=[pk(sg)], writes=[("thb", tb)])
                S.add("dve", lambda e, sv=sv, tb=tb, c=c: e.scalar_tensor_tensor(
                    out=awin(c, KA - 1), in0=v3(th[:, tb, 0:n]), scalar=1.0, in1=v3(pap(sv, 128, n)),
                    op0=ALU.add, op1=ALU.mult),
                    reads=[("thb", tb), pk(sv)], writes=[("aext", c)])
                if blk.last or smp:
                    c0, nc_ = (0, NS) if smp else (n - (KA - 1), KA - 1)
                    S.add("dve", lambda e, sv=sv, tb=tb, c=c, c0=c0, nc_=nc_: e.scalar_tensor_tensor(
                        out=a32[:, c, 0:nc_], in0=th[:, tb, c0:c0 + nc_], scalar=1.0,
                        in1=pap(sv, 128, n)[:, c0:c0 + nc_], op0=ALU.add, op1=ALU.mult),
                        reads=[("thb", tb), pk(sv)], writes=[("a32", c)])
            for c in range(3):
                bk = alloc_bank()
                sh = proj_fm(C_BH + c * 128, n, (bk, 0))
                scc = proj_fm(C_BC + c * 128, n, (bk, 1))
                tb = c % 2
                S.add("act", lambda e, sh=sh, tb=tb: e.activation(out=bh[:, tb, 0:n], in_=pap(sh, 128, n), func=AF.Copy),
                      reads=[pk(sh)], writes=[("bhb", tb)])
                S.add("dve", lambda e, scc=scc, tb=tb, c=c: e.tensor_tensor(
                    out=zwin(c, KB - 1), in0=v3(pap(scc, 128, n)), in1=v3(bh[:, tb, 0:n]), op=ALU.mult),
                    reads=[("bhb", tb), pk(scc)], writes=[("zext", c)])
                if blk.last:
                    S.add("dve", lambda e, scc=scc, tb=tb, c=c: e.tensor_tensor(
                        out=z32[:, c, 0:2], in0=pap(scc, 128, n)[:, n - 2:n], in1=bh[:, tb, n - 2:n], op=ALU.mult),
                        reads=[("bhb", tb), pk(scc)], writes=[("z32", c)])
                if smp:
                    S.add("dve", lambda e, scc=scc, tb=tb, c=c: e.tensor_tensor(
                        out=z32[:, c, 0:32].rearrange("p (g i) -> p g i", i=2),
                        in0=v3(pap(scc, 128, n))[:, :, 2:4], in1=v3(bh[:, tb, 0:n])[:, :, 2:4], op=ALU.mult),
                        reads=[("bhb", tb), pk(scc)], writes=[("z32", c)])
            if smp:
                pass
            else:
                ncol = n + 28
                nmv = ncol + 4
                for c in range(3):
                    for q in range(4):
                        b = 4 * c + q
                        bk = alloc_bank()
                        for j in range(4):
                            S.add("pe", lambda e, c=c, q=q, j=j, bk=bk: e.matmul(
                                pbank[bk][32 * j:32 * j + 32, 3 - j:3 - j + nmv], ident_bf[:, 32 * q:32 * q + 32],
                                a_ext[:, c, 0:nmv], start=True, stop=True, tile_position=(0, 32 * j)),
                                reads=[("aext", c), "ident_bf"], writes=[pk(bk)])
                        copy_op(A4[:, b, 0:ncol], pbank[bk][:, 3:3 + ncol], [pk(bk)], [("a4", b)])
            scv = []
            bkv = alloc_bank()
            for j, (slot, r) in enumerate(blk.tiles):
                s = (bkv, j)
                scv.append(s)
                pi = piece_of(C_CV)
                for k in range(8):
                    S.add("pe", lambda e, k=k, s=s, j=j, r=r: e.matmul(
                        pap(s, r, WC), h_fm[:, k, j * 128:j * 128 + r], w_in_sb[:, k, C_CV:C_CV + WC],
                        start=(k == 0), stop=(k == 7)),
                        reads=[("win", pi)] + HFM, writes=[pk(s)])
            for j, (slot, r) in enumerate(blk.tiles):
                s = scv[j]
                S.add("dve", lambda e, s=s, j=j, r=r: e.bn_stats(st6[0:r, j, :], pap(s, r, WC)),
                      reads=[pk(s)], writes=["st6"])
                S.add("dve", lambda e, j=j, r=r: e.bn_aggr(mv[0:r, j, :], st6[0:r, j, :]),
                      reads=["st6"], writes=["mv"])
            S.add("act", lambda e: e.activation(out=rstdc[0:rows, 0:nt], in_=mv[0:rows, 0:nt, 1], func=AF.Sqrt,
                                                bias=eps_col[0:rows, :], scale=1.0),
                  reads=["mv", "eps"], writes=["rstdc"])
            S.add("dve", lambda e: e.reciprocal(rstdc[0:rows, 0:nt], rstdc[0:rows, 0:nt]),
                  reads=["rstdc"], writes=["rstdc"])
            for j, (slot, r) in enumerate(blk.tiles):
                s = scv[j]
                S.add("dve", lambda e, s=s, j=j, r=r: e.tensor_scalar(
                    ncv[0:r, j, :], pap(s, r, WC), mv[0:r, j, 0:1], rstdc[0:r, j:j + 1], ALU.subtract, ALU.mult),
                    reads=[pk(s), "mv", "rstdc"], writes=[("ncv", j)])
                S.add("dve", lambda e, j=j, r=r: e.tensor_tensor(out=ncv[0:r, j, :], in0=ncv[0:r, j, :],
                                                                  in1=gC_b[0:r, :], op=ALU.mult),
                      reads=[("ncv", j), "gC_b"], writes=[("ncv", j)])
                if smp:
                    S.add("dve", lambda e, j=j, r=r: e.tensor_tensor(out=ncv[0:r, j, :], in0=ncv[0:r, j, :],
                                                                      in1=bC_b[0:r, :], op=ALU.add),
                          reads=[("ncv", j), "bC_b"], writes=[("ncv", j)])
                    S.add("sp", lambda e, r=r: e.dma_start(out=sv_s[l].rearrange("g i c -> (g i) c"),
                                                           in_=ncv[0:r, 0, :]),
                          reads=[("ncv", 0)], writes=[("sv_s", l)], dma=True)
                    S.add("act", lambda e, j=j, r=r: e.activation(out=vn_bf[0:r, j, :], in_=ncv[0:r, j, :],
                                                                  func=AF.Copy),
                          reads=[("ncv", j)], writes=[("vnbf", j)])
                else:
                    S.add("dve", lambda e, j=j, r=r: e.tensor_tensor(out=vn_bf[0:r, j, :], in0=ncv[0:r, j, :],
                                                                      in1=bC_b[0:r, :], op=ALU.add),
                          reads=[("ncv", j), "bC_b"], writes=[("vnbf", j)])

        def main_late(l, blk, next_blk, pre_conv=None, after_proj=None, h_next=None):
            n = blk.ntok
            smp = blk.sample
            if smp:
                aS = a_ext[:, :, 0:NSEG * 34].rearrange("p c (g r) -> p c g r", r=34)
                zS = z_ext[:, :, 0:NSEG * 6].rearrange("p c (g r) -> p c g r", r=6)

                def v3(ap2):
                    return ap2.rearrange("p (g i) -> p g i", i=DSEQ)

                def awin(c, k):
                    return aS[:, c, :, k:k + DSEQ]

                def zwin(c, k):
                    return zS[:, c, :, k:k + DSEQ]
            else:
                def v3(ap2):
                    return ap2

                def awin(c, k):
                    return a_ext[:, c, k:k + n]

                def zwin(c, k):
                    return z_ext[:, c, k:k + n]
            nt = len(blk.tiles)
            rows = blk.tiles[0][1]
            hn_blk, hn_l = (next_blk, l) if next_blk is not None else (h_next if h_next is not None else (None, l))
            if hn_blk is not None:
                H_elem(hn_blk)
            for c in range(3):
                bk = alloc_bank()
                sbb = proj_fm(C_BB + c * 128, n, (bk, 0))
                szs = proj_fm(C_BZ + c * 128, n, (bk, 1))
                tb = c % 2
                S.add("act", lambda e, szs=szs, tb=tb: e.activation(out=szb[:, tb, 0:n], in_=pap(szs, 128, n),
                                                                    func=AF.Silu),
                      reads=[pk(szs)], writes=[("szbb", tb)])
                for k in range(KB):
                    if k == 0:
                        S.add("dve", lambda e, c=c, k=k, tb=tb: e.tensor_scalar(
                            v3(bb[:, tb, 0:n]), zwin(c, k), colsA[:, c, KA + 3 + k:KA + 4 + k], None, ALU.mult),
                            reads=[("zext", c), "colsA"], writes=[("bbb", tb)])
                    else:
                        S.add("dve", lambda e, c=c, k=k, tb=tb: e.scalar_tensor_tensor(
                            out=v3(bb[:, tb, 0:n]), in0=zwin(c, k), scalar=colsA[:, c, KA + 3 + k:KA + 4 + k],
                            in1=v3(bb[:, tb, 0:n]), op0=ALU.mult, op1=ALU.add),
                            reads=[("zext", c), "colsA", ("bbb", tb)], writes=[("bbb", tb)])
                if not smp:
                    S.add("dve", lambda e, c=c: e.tensor_copy(z_ext[:, c, 0:KB - 1], z_ext[:, c, n:n + KB - 1]),
                          reads=[("zext", c)], writes=[("zext", c)])
                S.add("dve", lambda e, sbb=sbb, tb=tb: e.tensor_tensor(out=bb[:, tb, 0:n], in0=pap(sbb, 128, n),
                                                                       in1=bb[:, tb, 0:n], op=ALU.mult),
                      reads=[pk(sbb), ("bbb", tb)], writes=[("bbb", tb)])
                S.add("dve", lambda e, c=c, tb=tb: e.tensor_tensor(out=mix_fm[:, 3 + c, 0:n], in0=bb[:, tb, 0:n],
                                                                   in1=szb[:, tb, 0:n], op=ALU.mult),
                      reads=[("bbb", tb), ("szbb", tb)], writes=[("mix", 3 + c)])
            bkz = alloc_bank()
            cslots = []
            for hc in range(2):
                bk = alloc_bank()
                scm = (bk, 0)
                for j, (slot, r) in enumerate(blk.tiles):
                    for hh in range(2):
                        h = 2 * hc + hh
                        rhs = wblk[:, h, :] if smp else wmT[:, h, :]
                        S.add("pe", lambda e, j=j, r=r, hh=hh, h=h, rhs=rhs, scm=scm: e.matmul(
                            pbank[scm[0]][64 * hh:64 * hh + 64, j * 128:j * 128 + r],
                            vn_bf[0:r, j, h * 64:(h + 1) * 64], rhs, start=True, stop=True),
                            reads=[("vnbf", j), "wblk" if smp else "wmT"], writes=[pk(scm)])
                su = proj_fm(C_CU + hc * 128, n, (bk, 1))
                cslots.append((scm, su))
            szzs = [proj_fm(C_CZ + hc * 128, n, (bkz, hc)) for hc in range(2)]
            for hc in range(2):
                scm, su = cslots[hc]
                szz = szzs[hc]
                if smp:
                    cbv = cbias[:, hc, 0:DSEQ].unsqueeze(1).broadcast_to([128, NSEG, DSEQ])
                    pv = v3(pap(scm, 128, n))
                    tv = v3(tC[:, hc, 0:n])
                else:
                    cbv = cbias[:, hc, :].unsqueeze(1).broadcast_to([128, nt, 128])
                    pv = pap(scm, 128, n).rearrange("p (j t) -> p j t", t=128)
                    tv = tC[:, hc, 0:n].rearrange("p (j t) -> p j t", t=128)
                S.add("dve", lambda e, pv=pv, tv=tv, cbv=cbv: e.tensor_tensor(out=tv, in0=pv, in1=cbv, op=ALU.add),
                      reads=[pk(scm)] + [("cbias", h_) for h_ in range(4)], writes=[("tC", hc)])
                S.add("dve", lambda e, su=su, hc=hc: e.tensor_tensor(out=tC[:, hc, 0:n], in0=pap(su, 128, n),
                                                                     in1=tC[:, hc, 0:n], op=ALU.mult),
                      reads=[pk(su), ("tC", hc)], writes=[("tC", hc)])
                S.add("act", lambda e, szz=szz, hc=hc: e.activation(out=szc[:, hc, 0:n], in_=pap(szz, 128, n),
                                                                    func=AF.Silu),
                      reads=[pk(szz)], writes=[("szc", hc)])
                S.add("dve", lambda e, hc=hc: e.tensor_tensor(out=mix_fm[:, 6 + hc, 0:n], in0=tC[:, hc, 0:n],
                                                              in1=szc[:, hc, 0:n], op=ALU.mult),
                      reads=[("tC", hc), ("szc", hc)], writes=[("mix", 6 + hc)])
            bkaz = [alloc_bank(), alloc_bank()]
            szs_ = [proj_fm(C_AZ + c * 128, n, (bkaz[c // 2], c % 2)) for c in range(3)]
            for c in range(3):
                S.add("act", lambda e, c=c: e.activation(out=szt[:, c, 0:n], in_=pap(szs_[c], 128, n), func=AF.Silu),
                      reads=[pk(szs_[c])], writes=[("sztb", c)])
            if hn_blk is not None:
                H_pe(hn_blk)
                H_evac(hn_blk, hn_l)
            if after_proj is not None:
                after_proj()
            if pre_conv is not None:
                pre_conv()
            if smp:
                n_h = 34 * 7 + DSEQ
                ncol_s = n_h + 28
                nmv_s = ncol_s + 4
                for hf in range(2):
                    base = 272 * hf
                    for c in range(3):
                        for q in range(4):
                            b = 4 * c + q
                            bk = alloc_bank()
                            for j in range(4):
                                S.add("pe", lambda e, c=c, q=q, j=j, bk=bk, base=base: e.matmul(
                                    pbank[bk][32 * j:32 * j + 32, 3 - j:3 - j + nmv_s], ident_bf[:, 32 * q:32 * q + 32],
                                    a_ext[:, c, base:base + nmv_s], start=True, stop=True, tile_position=(0, 32 * j)),
                                    reads=[("aext", c), "ident_bf"], writes=[pk(bk)])
                            copy_op(A4[:, b, 0:ncol_s], pbank[bk][:, 3:3 + ncol_s], [pk(bk)], [("a4", b)])
                    for c in range(3):
                        bk = alloc_bank()
                        for m in range(8):
                            for q in range(4):
                                b = 4 * c + q
                                S.add("pe", lambda e, bk=bk, m=m, q=q, b=b: e.matmul(
                                    pbank[bk][32 * q:32 * q + 32, 0:n_h], W4[:, b, m, :], A4[:, b, 4 * m:4 * m + n_h],
                                    start=(m == 0), stop=(m == 7), tile_position=(0, 32 * q)),
                                    reads=["w4", ("a4", b)], writes=[pk(bk)])
                        src = pbank[bk][:, 0:272].rearrange("p (g r) -> p g r", r=34)[:, :, 0:DSEQ]
                        for dst_t, key, fn_ in ((cbh, "cbh", AF.Identity), (sq, "sq", AF.Square), (cb, "cb", AF.Identity)):
                            dst = dst_t[:, c, hf * 32:(hf + 1) * 32].rearrange("p (g i) -> p g i", i=DSEQ)
                            S.add("act", lambda e, dst=dst, src=src, fn_=fn_, c=c: e.activation(
                                out=dst, in_=src, func=fn_, bias=colsA[:, c, KA:KA + 1]),
                                reads=[pk(bk), "colsA", (key, c)], writes=[(key, c)])
            for c in (range(3) if not smp else ()):
                bk = alloc_bank()
                for m in range(8):
                    for q in range(4):
                        b = 4 * c + q
                        S.add("pe", lambda e, bk=bk, m=m, q=q, b=b: e.matmul(
                            pbank[bk][32 * q:32 * q + 32, 0:n], W4[:, b, m, :], A4[:, b, 4 * m:4 * m + n],
                            start=(m == 0), stop=(m == 7), tile_position=(0, 32 * q)),
                            reads=["w4", ("a4", b)], writes=[pk(bk)])
                if not blk.last:
                    S.add("dve", lambda e, c=c: e.tensor_copy(a_ext[:, c, 0:KA - 1], a_ext[:, c, n:n + KA - 1]),
                          reads=[("aext", c)], writes=[("aext", c)])
                S.add("act", lambda e, c=c, bk=bk: e.activation(out=cbh[:, c, 0:n], in_=pbank[bk][:, 0:n],
                                                                func=AF.Identity, bias=colsA[:, c, KA:KA + 1]),
                      reads=[pk(bk), "colsA"], writes=[("cbh", c)])
                S.add("act", lambda e, c=c, bk=bk: e.activation(out=sq[:, c, 0:n], in_=pbank[bk][:, 0:n],
                                                                func=AF.Square, bias=colsA[:, c, KA:KA + 1]),
                      reads=[pk(bk), "colsA"], writes=[("sq", c)])
                S.add("dve", lambda e, c=c, bk=bk: e.tensor_scalar(cb[:, c, 0:n], pbank[bk][:, 0:n],
                                                                   colsA[:, c, KA:KA + 1], None, ALU.add),
                      reads=[pk(bk), "colsA"], writes=[("cb", c)])
            def state_out():
                if blk.last:
                    so = alloc_bank()
                    for c in range(3):
                        pe_transpose(pfull(so, KA - 1, 512)[:, c * 128:(c + 1) * 128], a32[:, c, 0:KA - 1], ident_f[:],
                                     [("a32", c), "ident_f"], [pk(so)])
                    stg = tC[:].rearrange("p a b -> p (a b)")
                    copy_op(stg[0:KA - 1, 0:WA], pfull(so, KA - 1, WA), [pk(so)], [("tC", 0), ("tC", 1)],
                            scale=0.5, eng="act")
                    S.add("sp", lambda e: e.dma_start(out=sa_p[l], in_=stg[0:KA - 1, 0:WA]),
                          reads=[("tC", 0), ("tC", 1)], writes=[("sa_p", l)], dma=True)
                    so2 = alloc_bank()
                    for c in range(3):
                        pe_transpose(pfull(so2, 2, 512)[:, c * 128:(c + 1) * 128], z32[:, c, 0:2], ident_f[:],
                                     [("z32", c), "ident_f"], [pk(so2)])
                    stg2 = szc[:].rearrange("p a b -> p (a b)")
                    copy_op(stg2[0:2, 0:WA], pfull(so2, 2, WA), [pk(so2)], [("szc", 0), ("szc", 1)], eng="act")
                    S.add("sp", lambda e: e.dma_start(out=sb_p[l], in_=stg2[0:2, 0:WA]),
                          reads=[("szc", 0), ("szc", 1)], writes=[("sb_p", l)], dma=True)
                if smp:
                    stg = tC[:].rearrange("p a b -> p (a b)")
                    for i in range(DSEQ):
                        so = alloc_bank()
                        for c in range(3):
                            pe_transpose(pfull(so, NSEG, 512)[:, c * 128:(c + 1) * 128],
                                         a32[:, c, 0:NS].rearrange("p (g i) -> p g i", i=DSEQ)[:, :, i], ident_f[:],
                                         [("a32", c), "ident_f"], [pk(so)])
                        copy_op(stg[0:NSEG, 0:WA], pfull(so, NSEG, WA), [pk(so)], [("tC", 0), ("tC", 1)],
                                scale=0.5, eng="act")
                        S.add("sp", lambda e, i=i: e.dma_start(out=sa_s[l, :, KA - 1 - DSEQ + i, :], in_=stg[0:NSEG, 0:WA]),
                              reads=[("tC", 0), ("tC", 1)], writes=[("sa_s", l, i)], dma=True)
                    so2 = alloc_bank()
                    for c in range(3):
                        pe_transpose(pfull(so2, 32, 512)[:, c * 128:(c + 1) * 128], z32[:, c, 0:32], ident_f[:],
                                     [("z32", c), "ident_f"], [pk(so2)])
                    stg2 = szc[:].rearrange("p a b -> p (a b)")
                    copy_op(stg2[0:32, 0:WA], pfull(so2, 32, WA), [pk(so2)], [("szc", 0), ("szc", 1)], eng="act")
                    S.add("sp", lambda e: e.dma_start(out=sb_s[l].rearrange("g r c -> (g r) c"), in_=stg2[0:32, 0:WA]),
                          reads=[("szc", 0), ("szc", 1)], writes=[("sb_s", l)], dma=True)

            if blk.last:
                state_out()
            bks = alloc_bank()
            s1 = (bks, 0)
            for c in range(3):
                S.add("pe", lambda e, c=c: e.matmul(pap(s1, 128, n), ones_bf[:], cbh[:, c, 0:n],
                                                    start=(c == 0), stop=(c == 2)),
                      reads=["ones_bf", ("cbh", c)], writes=[pk(s1)])
            s2 = (bks, 1)
            for c in range(3):
                S.add("pe", lambda e, c=c: e.matmul(pap(s2, 128, n), ones_bf[:], sq[:, c, 0:n],
                                                    start=(c == 0), stop=(c == 2)),
                      reads=["ones_bf", ("sq", c)], writes=[pk(s2)])
            S.add("act", lambda e: e.activation(out=mean_b[:, 0:n], in_=pap(s1, 128, n), func=AF.Copy, scale=1.0 / WA),
                  reads=[pk(s1)], writes=["mean_b"])
            S.add("dve", lambda e: e.tensor_tensor(out=vb[:, 0:n], in0=mean_b[:, 0:n], in1=mean_b[:, 0:n], op=ALU.mult),
                  reads=["mean_b"], writes=["vb"])
            S.add("dve", lambda e: e.scalar_tensor_tensor(out=vb[:, 0:n], in0=pap(s2, 128, n), scalar=1.0 / WA,
                                                          in1=vb[:, 0:n], op0=ALU.mult, op1=ALU.subtract),
                  reads=[pk(s2), "vb"], writes=["vb"])
            S.add("act", lambda e: e.activation(out=vb[:, 0:n], in_=vb[:, 0:n], func=AF.Sqrt, bias=eps_col[:, :],
                                                scale=1.0),
                  reads=["vb", "eps"], writes=["vb"])
            S.add("dve", lambda e: e.reciprocal(vb[:, 0:n], vb[:, 0:n]), reads=["vb"], writes=["vb"])
            for c in range(3):
                S.add("dve", lambda e, c=c: e.tensor_tensor(out=cb[:, c, 0:n], in0=cb[:, c, 0:n], in1=mean_b[:, 0:n],
                                                            op=ALU.subtract),
                      reads=[("cb", c), "mean_b"], writes=[("cb", c)])
                S.add("dve", lambda e, c=c: e.tensor_tensor(out=cb[:, c, 0:n], in0=cb[:, c, 0:n], in1=vb[:, 0:n],
                                                            op=ALU.mult),
                      reads=[("cb", c), "vb"], writes=[("cb", c)])
                S.add("act", lambda e, c=c: e.activation(out=cb[:, c, 0:n], in_=cb[:, c, 0:n], func=AF.Silu,
                                                         bias=colsA[:, c, KA + 2:KA + 3], scale=colsA[:, c, KA + 1:KA + 2]),
                      reads=[("cb", c), "colsA"], writes=[("cb", c)])
                S.add("dve", lambda e, c=c: e.tensor_tensor(out=mix_fm[:, c, 0:n], in0=cb[:, c, 0:n],
                                                            in1=szt[:, c, 0:n], op=ALU.mult),
                      reads=[("cb", c), ("sztb", c)], writes=[("mix", c)])
            if smp:
                state_out()

        MIX = [("mix", k) for k in range(8)]

        def O_stage(l, blk):
            for j, (slot, r) in enumerate(blk.tiles):
                for hf in range(2):
                    so = alloc_bank()
                    for k in range(8):
                        S.add("pe", lambda e, k=k, so=so, j=j, r=r, hf=hf: e.matmul(
                            pfull(so, r, 512), mix_fm[:, k, j * 128:j * 128 + r],
                            w_out_sb[:, k, hf * 512:(hf + 1) * 512], start=(k == 0), stop=(k == 7)),
                            reads=MIX + [("wout", hf)], writes=[pk(so)])
                    S.add("dve", lambda e, so=so, slot=slot, r=r, hf=hf: e.tensor_tensor(
                        out=x_all[0:r, slot, hf * 512:(hf + 1) * 512], in0=pfull(so, r, 512),
                        in1=x_all[0:r, slot, hf * 512:(hf + 1) * 512], op=ALU.add),
                        reads=[pk(so), ("x", slot)], writes=[("x", slot)])

        def final_stage(blk):
            nt = len(blk.tiles)
            rows = blk.tiles[0][1]
            for j, (slot, r) in enumerate(blk.tiles):
                S.add("act", lambda e, j=j, slot=slot, r=r: e.activation(
                    out=h_tm[0:r, j, :], in_=x_all[0:r, slot, :], func=AF.Square, accum_out=ss[0:r, j:j + 1]),
                    reads=[("x", slot)], writes=[("htm", j), "ss"], multi=True)
            rstd_from_sumsq(rows, nt)
            for j, (slot, r) in enumerate(blk.tiles):
                S.add("dve", lambda e, j=j, slot=slot, r=r: e.scalar_tensor_tensor(
                    out=x_all[0:r, slot, :], in0=x_all[0:r, slot, :], scalar=rstd[0:r, j:j + 1], in1=fg_b[0:r, :],
                    op0=ALU.mult, op1=ALU.mult),
                    reads=[("x", slot), "rstd", "fg_b"], writes=[("x", slot)])
                if blk.sample:
                    S.add("sp", lambda e, slot=slot, r=r: e.dma_start(out=y_s[:, :], in_=x_all[0:r, slot, :]),
                          reads=[("x", slot)], writes=[("y", slot)], dma=True)
                else:
                    S.add("sp", lambda e, slot=slot, r=r: e.dma_start(out=y_p[slot * 128:(slot + 1) * 128, :],
                                                                      in_=x_all[0:r, slot, :]),
                          reads=[("x", slot)], writes=[("y", slot)], dma=True)

        eps_col = sb("eps_col", [128, 1])
        S.add("pool", lambda e: e.memset(eps_col[:], EPS), writes=["eps"])

        blocks = []
        for b in range(npb):
            blocks.append(Blk(NB, [(2 * b, 128), (2 * b + 1, 128)], False, b == 0, b == npb - 1, b))
        blocks.append(Blk(NS, [(16, NS)], True, False, False, -1))

        for l in range(depth):
            layer_setup_A(l)
            if l == 0:
                load_x(4, 17, q_="pool")
            if l == 0:
                layer_setup_Bdma(l)
            if l == 0:
                H_elem(blocks[0])
                H_pe(blocks[0])
                H_evac(blocks[0], l)
            pre_conv = layer_setup_Bparams(l)
            main_early(l, blocks[0])
            pre_conv()
            pre_conv = None
            layer_setup_Bcomp(l)
            if l + 1 < depth:
                weight_prestage(l + 1)
            for bi, blk in enumerate(blocks):
                nxt = blocks[bi + 1] if bi + 1 < len(blocks) else None
                refill = (nxt is None and l + 1 < depth)
                if refill:
                    mid = (lambda: weight_loads(l + 1, ("late",)))
                elif nxt is not None and nxt.sample:
                    mid = (lambda: sample_hist_z(l))
                else:
                    mid = None
                main_late(l, blk, nxt, pre_conv if bi == 0 else None, mid,
                          (blocks[0], l + 1) if refill else None)
                if nxt is not None:
                    if nxt.last:
                        sample_hist_a_dma(l)
                    main_early(l, nxt)
                    if nxt.last:
                        sample_hist_a_pe(l)
                        sample_hist_z_dma(l)
                    if bi + 2 == len(blocks) and l + 1 < depth:
                        weight_loads(l + 1, ("early",))
                if refill:
                    layer_setup_Bdma(l + 1, "pool")
                O_stage(l, blk)
                if refill:
                    weight_loads(l + 1, ("wout",))
                if l == depth - 1:
                    final_stage(blk)

        with nc.Block() as block:
            S.emit(nc, stack, block)
    return nc


_CACHE = {}


def _get_program():
    if "nc" not in _CACHE:
        _CACHE["nc"] = build_program()
    return _CACHE["nc"]


def kernel(x_prompt, x_sample, state_a_conv, state_b_conv, norm_g, w_in, a_conv_w, a_conv_b, a_ln_g, a_ln_b,
           b_conv_w, c_ln_g, c_ln_b, c_ws, c_bs, w_out, final_g):
    f = lambda a: np.ascontiguousarray(np.asarray(a, dtype=np.float32))
    x_prompt, x_sample = f(x_prompt), f(x_sample)
    state_a_conv, state_b_conv = f(state_a_conv), f(state_b_conv)
    shared = {
        "norm_g": f(norm_g).reshape(DEPTH, 8, 128),
        "w_in": f(w_in),
        "a_conv_w": f(a_conv_w),
        "a_conv_b": f(a_conv_b).reshape(DEPTH, 1, WA),
        "a_ln_g": f(a_ln_g).reshape(DEPTH, 1, WA),
        "a_ln_b": f(a_ln_b).reshape(DEPTH, 1, WA),
        "b_conv_w": f(b_conv_w),
        "c_ln_g": f(c_ln_g).reshape(DEPTH, 1, WC),
        "c_ln_b": f(c_ln_b).reshape(DEPTH, 1, WC),
        "c_ws": f(c_ws),
        "c_bs": f(c_bs),
        "w_out": f(w_out),
        "final_g": f(final_g).reshape(1, D),
    }
    in_maps = []
    for i in range(NCORES):
        m = dict(shared)
        m["x_p"] = x_prompt[i]
        m["x_s"] = np.ascontiguousarray(x_sample[NSEG * i:NSEG * (i + 1)].reshape(NS, D))
        m["st_a"] = np.ascontiguousarray(state_a_conv[:, NSEG * i:NSEG * (i + 1)])
        m["st_b"] = np.ascontiguousarray(state_b_conv[:, NSEG * i:NSEG * (i + 1)])
        in_maps.append(m)
    nc = _get_program()
    res = run_bass_kernel_spmd(nc, in_maps, core_ids=list(range(NCORES)))
    R = res.results
    y_prompt = np.stack([R[i]["y_p"] for i in range(NCORES)], axis=0)
    y_sample = np.concatenate([R[i]["y_s"].reshape(NSEG, DSEQ, D) for i in range(NCORES)], axis=0)
    sa_p = np.stack([R[i]["sa_p"] for i in range(NCORES)], axis=1)
    sb_p = np.stack([R[i]["sb_p"] for i in range(NCORES)], axis=1)
    sa_s = np.concatenate([R[i]["sa_s"] for i in range(NCORES)], axis=1)
    sb_s = np.concatenate([R[i]["sb_s"] for i in range(NCORES)], axis=1)
    sv_s = np.concatenate([R[i]["sv_s"] for i in range(NCORES)], axis=1)
    out = (y_prompt, y_sample, sa_p, sb_p, sa_s, sb_s, sv_s)
    return tuple(np.ascontiguousarray(o, dtype=np.float32) for o in out)
```
